# Optimizing a Trainium2 kernel written in Bass

```python
import math
import jax, jax.numpy as jnp
from jax import lax
import numpy as np

D_MODEL = 1024
BATCH = 2
SEQ = 16384
DEPTH = 2

CTX_LEN = 256
GRID_W = 64
Q_BLOCK = 128
HEAD_DIM = 64
GROUP_WIDTH = D_MODEL // 4
EPS = 1e-6
ROPE_THETA = 10000.0
HALF = 0.5
N_MOD = 9

DA_HEADS = GROUP_WIDTH // HEAD_DIM
DA_QK_DIM = HEAD_DIM // 2
DA_V_DIM = HEAD_DIM
MLA_HEADS = GROUP_WIDTH // HEAD_DIM
MLA_Q_RANK = GROUP_WIDTH
MLA_KV_RANK = GROUP_WIDTH // 2
MLA_NOPE_DIM = HEAD_DIM
MLA_ROPE_DIM = HEAD_DIM // 2
MLA_V_DIM = HEAD_DIM
HG_HEADS = GROUP_WIDTH // HEAD_DIM
HG_EXPAND = 128
HG_FDIM = HG_HEADS * HG_EXPAND
HG_VDIM = GROUP_WIDTH // HG_HEADS
HG_CHUNK = 64
GQ_HEADS = GROUP_WIDTH // HEAD_DIM
GQ_KV_HEADS = 2
D_FF = 2816

A_COLS = 4 * DA_HEADS * DA_QK_DIM + DA_HEADS * DA_V_DIM
B_COLS = MLA_Q_RANK + MLA_KV_RANK + MLA_ROPE_DIM
C_COLS = 3 * HG_FDIM + 2 * GROUP_WIDTH
D_COLS = (GQ_HEADS + 2 * GQ_KV_HEADS) * HEAD_DIM
MIX_SPLITS = (A_COLS, B_COLS, C_COLS, D_COLS)
MIX_COLS = A_COLS + B_COLS + C_COLS + D_COLS

kernel_name = "hybrid_parallel_group_dit_block"


def _rms(x, w):
    xf = x.astype(jnp.float32)
    y = xf * lax.rsqrt(jnp.mean(xf * xf, axis=-1, keepdims=True) + EPS)
    return (y * w.astype(jnp.float32)).astype(x.dtype)


def _split_cols(z, sizes):
    idx = np.cumsum(np.array(sizes))[:-1].tolist()
    return jnp.split(z, idx, axis=-1)


def _heads(x, h):
    b, n, _ = x.shape
    return x.reshape(b, n, h, -1).transpose(0, 2, 1, 3)


def _merge(x):
    b, h, n, d = x.shape
    return x.transpose(0, 2, 1, 3).reshape(b, n, h * d)


def _axial_rope(n, rot_dim, dtype):
    rows = n // GRID_W
    r = jnp.repeat(jnp.arange(rows), GRID_W).astype(jnp.float32)
    col = jnp.tile(jnp.arange(GRID_W), rows).astype(jnp.float32)
    n_freq = rot_dim // 4
    inv = ROPE_THETA ** (-jnp.arange(n_freq, dtype=jnp.float32) / n_freq)
    ang = jnp.concatenate([r[:, None] * inv, col[:, None] * inv], axis=-1)
    return jnp.cos(ang).astype(dtype), jnp.sin(ang).astype(dtype)


def _rope(x, cos, sin):
    x1, x2 = jnp.split(x, 2, axis=-1)
    return jnp.concatenate([x1 * cos - x2 * sin, x1 * sin + x2 * cos], axis=-1)


def _attend(q, k, v, scale):
    s = jnp.einsum("bhgqd,bhkd->bhgqk", q, k, preferred_element_type=jnp.float32) * scale
    p = jax.nn.softmax(s, axis=-1).astype(v.dtype)
    return jnp.einsum("bhgqk,bhkv->bhgqv", p, v)


def _block_attend(q, k, v, scale):
    b, hk, g, n, d = q.shape
    nb = n // Q_BLOCK
    qb = jnp.moveaxis(q.reshape(b, hk, g, nb, Q_BLOCK, d), 3, 0)
    ob = lax.map(lambda qi: _attend(qi, k, v, scale), qb)
    return jnp.moveaxis(ob, 0, 3).reshape(b, hk, g, n, v.shape[-1])


def _diff_attention(zl, zc, qk_norm, lam_p, subln, layer_idx, need_ctx):
    lam_init = 0.8 - 0.6 * math.exp(-0.3 * layer_idx)
    lp = lam_p.astype(jnp.float32)
    lam = jnp.exp(jnp.sum(lp[0] * lp[1])) - jnp.exp(jnp.sum(lp[2] * lp[3])) + lam_init
    scale = DA_QK_DIM ** -0.5

    def qkv(z):
        q, k, v = _split_cols(z, (2 * DA_HEADS * DA_QK_DIM, 2 * DA_HEADS * DA_QK_DIM, DA_HEADS * DA_V_DIM))
        q = _rms(_heads(q, 2 * DA_HEADS), qk_norm[0])
        k = _rms(_heads(k, 2 * DA_HEADS), qk_norm[1])
        v = jnp.repeat(_heads(v, DA_HEADS), 2, axis=1)
        return q, k, v

    def combine(o):
        b, _, n, dv = o.shape
        o = o.reshape(b, DA_HEADS, 2, n, dv)
        o = o[:, :, 0] - lam.astype(o.dtype) * o[:, :, 1]
        return _merge(_rms(o, subln) * (1.0 - lam_init))

    cos, sin = _axial_rope(zl.shape[1], DA_QK_DIM, zl.dtype)
    ql, kl, vl = qkv(zl)
    ql, kl = _rope(ql, cos, sin), _rope(kl, cos, sin)
    qc, kc, vc = qkv(zc)
    keys = jnp.concatenate([kc, kl], axis=2)
    vals = jnp.concatenate([vc, vl], axis=2)
    yl = combine(_block_attend(ql[:, :, None], keys, vals, scale)[:, :, 0])
    yc = combine(_attend(qc[:, :, None], kc, vc, scale)[:, :, 0]) if need_ctx else None
    return yl, yc


def _mla(zl, zc, q_norm, kv_norm, w_uq, w_ukv, nope_norm, rope_norm, need_ctx):
    scale = (MLA_NOPE_DIM + MLA_ROPE_DIM) ** -0.5

    def qkv(z, tabs):
        cq, ckv, kr = _split_cols(z, (MLA_Q_RANK, MLA_KV_RANK, MLA_ROPE_DIM))
        q = _heads(_rms(cq, q_norm) @ w_uq, MLA_HEADS)
        kv = _heads(_rms(ckv, kv_norm) @ w_ukv, MLA_HEADS)
        q_nope, q_rope = jnp.split(q, [MLA_NOPE_DIM], axis=-1)
        k_nope, v = jnp.split(kv, [MLA_NOPE_DIM], axis=-1)
        q_nope = _rms(q_nope, nope_norm[0])
        k_nope = _rms(k_nope, nope_norm[1])
        q_rope = _rms(q_rope, rope_norm[0])
        k_rope = _rms(kr[:, None], rope_norm[1])
        if tabs is not None:
            q_rope = _rope(q_rope, *tabs)
            k_rope = _rope(k_rope, *tabs)
        b, h, n, _ = k_nope.shape
        q = jnp.concatenate([q_nope, q_rope], axis=-1)
        k = jnp.concatenate([k_nope, jnp.broadcast_to(k_rope, (b, h, n, MLA_ROPE_DIM))], axis=-1)
        return q, k, v

    ql, kl, vl = qkv(zl, _axial_rope(zl.shape[1], MLA_ROPE_DIM, zl.dtype))
    qc, kc, vc = qkv(zc, None)
    keys = jnp.concatenate([kc, kl], axis=2)
    vals = jnp.concatenate([vc, vl], axis=2)
    yl = _merge(_block_attend(ql[:, :, None], keys, vals, scale)[:, :, 0])
    yc = _merge(_attend(qc[:, :, None], kc, vc, scale)[:, :, 0]) if need_ctx else None
    return yl, yc


def _chunk_scan(q, k, v, logf, s0):
    b, h, n, dk = q.shape
    dv = v.shape[-1]
    nc = n // HG_CHUNK

    def chunks(a):
        return jnp.moveaxis(a.reshape(b, h, nc, HG_CHUNK, a.shape[-1]), 2, 0)

    tri = jnp.tril(jnp.ones((HG_CHUNK, HG_CHUNK), dtype=bool))[:, :, None]

    def step(S, blk):
        qb, kb, vb, gb = blk
        G = jnp.cumsum(gb, axis=2)
        rel = jnp.where(tri, G[:, :, :, None, :] - G[:, :, None, :, :], -jnp.inf)
        att = jnp.einsum("bhtsk,bhsk->bhts", qb[:, :, :, None, :] * jnp.exp(rel), kb)
        o = jnp.einsum("bhts,bhsv->bhtv", att, vb) + jnp.einsum("bhtk,bhkv->bhtv", qb * jnp.exp(G), S)
        G_end = G[:, :, -1:, :]
        S = jnp.exp(G_end[:, :, 0, :, None]) * S + jnp.einsum("bhsk,bhsv->bhkv", kb * jnp.exp(G_end - G), vb)
        return S, o

    S, o = lax.scan(step, s0, (chunks(q), chunks(k), chunks(v), chunks(logf)))
    return jnp.moveaxis(o, 0, 2).reshape(b, h, n, dv), S


def _hgrn2(zl, zc, lb, out_norm, need_ctx):
    def feats(z):
        q, xf, xb, i, g = _split_cols(z, (HG_FDIM, HG_FDIM, HG_FDIM, GROUP_WIDTH, GROUP_WIDTH))
        q = _heads(jax.nn.silu(q), HG_HEADS).astype(jnp.float32)
        v = _heads(i, HG_HEADS).astype(jnp.float32)

        def gate(xd, lbd):
            xd = _heads(xd, HG_HEADS).astype(jnp.float32)
            lbd = lbd.reshape(HG_HEADS, 1, HG_EXPAND)
            logf = jnp.logaddexp(jnp.log(lbd), jnp.log1p(-lbd) + jax.nn.log_sigmoid(xd))
            kk = (1.0 - lbd) * jax.nn.sigmoid(-xd)
            return kk, logf
        return q, v, gate(xf, lb[0]), gate(xb, lb[1]), g

    def flip(a):
        return jnp.flip(a, axis=2)

    def bidir(q, v, fw, bw, s_f, s_b):
        o_f, st_f = _chunk_scan(q, fw[0], v, fw[1], s_f)
        o_b, st_b = _chunk_scan(flip(q), flip(bw[0]), flip(v), flip(bw[1]), s_b)
        return o_f + flip(o_b), st_f, st_b

    def readout(o, g):
        return _merge(_rms(o.astype(g.dtype), out_norm)) * jax.nn.silu(g)

    qc, vc, fwc, bwc, gc = feats(zc)
    b = zc.shape[0]
    s0 = jnp.zeros((b, HG_HEADS, HG_EXPAND, HG_VDIM), jnp.float32)
    oc, sc_f, sc_b = bidir(qc, vc, fwc, bwc, s0, s0)
    ql, vl, fwl, bwl, gl = feats(zl)
    ol, _, _ = bidir(ql, vl, fwl, bwl, sc_f, sc_b)
    yl = readout(ol, gl)
    yc = readout(oc, gc) if need_ctx else None
    return yl, yc


def _gqa(zl, zc, qk_norm, need_ctx):
    scale = HEAD_DIM ** -0.5
    grp = GQ_HEADS // GQ_KV_HEADS

    def qkv(z, tabs):
        q, k, v = _split_cols(z, (GQ_HEADS * HEAD_DIM, GQ_KV_HEADS * HEAD_DIM, GQ_KV_HEADS * HEAD_DIM))
        q = _rms(_heads(q, GQ_HEADS), qk_norm[0])
        k = _rms(_heads(k, GQ_KV_HEADS), qk_norm[1])
        v = _heads(v, GQ_KV_HEADS)
        if tabs is not None:
            q, k = _rope(q, *tabs), _rope(k, *tabs)
        b, _, n, d = q.shape
        return q.reshape(b, GQ_KV_HEADS, grp, n, d), k, v

    def out(o):
        b, hk, g, n, dv = o.shape
        return _merge(o.reshape(b, hk * g, n, dv))

    ql, kl, vl = qkv(zl, _axial_rope(zl.shape[1], HEAD_DIM, zl.dtype))
    qc, kc, vc = qkv(zc, None)
    keys = jnp.concatenate([kc, kl], axis=2)
    vals = jnp.concatenate([vc, vl], axis=2)
    yl = out(_block_attend(ql, keys, vals, scale))
    yc = out(_attend(qc, kc, vc, scale)) if need_ctx else None
    return yl, yc


def _swiglu(h, w_in, w_out):
    g, u = jnp.split(h @ w_in, 2, axis=-1)
    return (jax.nn.silu(g) * u) @ w_out


def _layer(xl, xc, c, c_ctx, layer_idx, last, ada_w, ada_b, norm_w, ffn_w_in, ffn_w_out,
           mix_w_in, mix_w_out, da_qk_norm, da_lambda, da_subln, mla_q_norm, mla_kv_norm,
           mla_w_uq, mla_w_ukv, mla_nope_norm, mla_rope_norm, hg_lb, hg_out_norm, gq_qk_norm):
    need_ctx = not last
    mod = (jax.nn.silu(c) @ ada_w + ada_b).reshape(c.shape[0], N_MOD, 1, D_MODEL)
    mod_c = (jax.nn.silu(c_ctx) @ ada_w + ada_b).reshape(1, N_MOD, 1, D_MODEL)

    def modnorm(x, m, i):
        return _rms(x, norm_w[i]) * (1.0 + m[:, 3 * i + 1]) + m[:, 3 * i]

    xl = xl + HALF * mod[:, 2] * _swiglu(modnorm(xl, mod, 0), ffn_w_in[0], ffn_w_out[0])
    xc = xc + HALF * mod_c[:, 2] * _swiglu(modnorm(xc, mod_c, 0), ffn_w_in[0], ffn_w_out[0])

    zl = modnorm(xl, mod, 1) @ mix_w_in
    zc = modnorm(xc, mod_c, 1) @ mix_w_in
    la, lb, lc, ld = _split_cols(zl, MIX_SPLITS)
    ca, cb, cc, cd = _split_cols(zc, MIX_SPLITS)
    ya_l, ya_c = _diff_attention(la, ca, da_qk_norm, da_lambda, da_subln, layer_idx, need_ctx)
    yb_l, yb_c = _mla(lb, cb, mla_q_norm, mla_kv_norm, mla_w_uq, mla_w_ukv, mla_nope_norm, mla_rope_norm, need_ctx)
    yc_l, yc_c = _hgrn2(lc, cc, hg_lb, hg_out_norm, need_ctx)
    yd_l, yd_c = _gqa(ld, cd, gq_qk_norm, need_ctx)
    xl = xl + mod[:, 5] * (jnp.concatenate([ya_l, yb_l, yc_l, yd_l], axis=-1) @ mix_w_out)
    if need_ctx:
        xc = xc + mod_c[:, 5] * (jnp.concatenate([ya_c, yb_c, yc_c, yd_c], axis=-1) @ mix_w_out)

    xl = xl + HALF * mod[:, 8] * _swiglu(modnorm(xl, mod, 2), ffn_w_in[1], ffn_w_out[1])
    if need_ctx:
        xc = xc + HALF * mod_c[:, 8] * _swiglu(modnorm(xc, mod_c, 2), ffn_w_in[1], ffn_w_out[1])
    return xl, xc


def setup_inputs(seed: int = 0) -> dict:
    key = jax.random.key(seed)
    ks = iter(jax.random.split(key, 32))

    def nrm(shape, s):
        return jax.random.normal(next(ks), shape, jnp.float32) * s

    def gain(shape):
        return 1.0 + nrm(shape, 0.02)

    return {
        "x": nrm((BATCH, SEQ, D_MODEL), 1.0),
        "c": nrm((BATCH, D_MODEL), 1.0),
        "ctx": nrm((BATCH, CTX_LEN, D_MODEL), 1.0),
        "c_ctx": nrm((D_MODEL,), 1.0),
        "ada_w": nrm((DEPTH, D_MODEL, N_MOD * D_MODEL), 0.5 * D_MODEL ** -0.5),
        "ada_b": nrm((DEPTH, N_MOD * D_MODEL), 0.01),
        "norm_w": gain((DEPTH, 3, D_MODEL)),
        "ffn_w_in": nrm((DEPTH, 2, D_MODEL, 2 * D_FF), D_MODEL ** -0.5),
        "ffn_w_out": nrm((DEPTH, 2, D_FF, D_MODEL), D_FF ** -0.5),
        "mix_w_in": nrm((DEPTH, D_MODEL, MIX_COLS), D_MODEL ** -0.5),
        "mix_w_out": nrm((DEPTH, D_MODEL, D_MODEL), D_MODEL ** -0.5),
        "da_qk_norm": gain((DEPTH, 2, DA_QK_DIM)),
        "da_lambda": nrm((DEPTH, 4, DA_QK_DIM), 0.1),
        "da_subln": gain((DEPTH, DA_V_DIM)),
        "mla_q_norm": gain((DEPTH, MLA_Q_RANK)),
        "mla_kv_norm": gain((DEPTH, MLA_KV_RANK)),
        "mla_w_uq": nrm((DEPTH, MLA_Q_RANK, MLA_HEADS * (MLA_NOPE_DIM + MLA_ROPE_DIM)), MLA_Q_RANK ** -0.5),
        "mla_w_ukv": nrm((DEPTH, MLA_KV_RANK, MLA_HEADS * (MLA_NOPE_DIM + MLA_V_DIM)), MLA_KV_RANK ** -0.5),
        "mla_nope_norm": gain((DEPTH, 2, MLA_NOPE_DIM)),
        "mla_rope_norm": gain((DEPTH, 2, MLA_ROPE_DIM)),
        "hg_lb_logits": nrm((2, DEPTH, HG_FDIM), 0.1),
        "hg_out_norm": gain((DEPTH, HG_VDIM)),
        "gq_qk_norm": gain((DEPTH, 2, HEAD_DIM)),
    }


def reference(x, c, ctx, c_ctx, ada_w, ada_b, norm_w, ffn_w_in, ffn_w_out, mix_w_in, mix_w_out,
              da_qk_norm, da_lambda, da_subln, mla_q_norm, mla_kv_norm, mla_w_uq, mla_w_ukv,
              mla_nope_norm, mla_rope_norm, hg_lb_logits, hg_out_norm, gq_qk_norm):
    p = jax.nn.softmax(hg_lb_logits.astype(jnp.float32), axis=1)
    lbs = jnp.maximum(jnp.cumsum(p, axis=1) - p[:, :1], 0.0)
    xl, xc = x, ctx
    for l in range(DEPTH):
        xl, xc = _layer(xl, xc, c, c_ctx, l, l == DEPTH - 1, ada_w[l], ada_b[l], norm_w[l],
                        ffn_w_in[l], ffn_w_out[l], mix_w_in[l], mix_w_out[l], da_qk_norm[l],
                        da_lambda[l], da_subln[l], mla_q_norm[l], mla_kv_norm[l], mla_w_uq[l],
                        mla_w_ukv[l], mla_nope_norm[l], mla_rope_norm[l], lbs[:, l],
                        hg_out_norm[l], gq_qk_norm[l])
    return xl
```

```python
import numpy as np
import ml_dtypes
import concourse.bass as bass
import concourse.mybir as mybir
from concourse.bass_utils import run_bass_kernel_spmd

F32 = mybir.dt.float32
BF16 = mybir.dt.bfloat16
AF = mybir.ActivationFunctionType
ALU = mybir.AluOpType
AX = mybir.AxisListType

D = 1024
DFF = 2816
MIXC = 3744
EPS = 1e-6
NCORES = 8


class Res:
    __slots__ = ("name", "w", "rd")

    def __init__(self, name):
        self.name = name
        self.w = None
        self.rd = {}


class Sem:
    __slots__ = ("h", "count", "name")

    def __init__(self, h, name):
        self.h = h
        self.count = 0
        self.name = name


class Buf:
    def __init__(self, ap, name, ds=None):
        self.ap = ap
        self.r = Res(name)
        self.ds = ds

    def __getitem__(self, k):
        return self.ap[k]


class _Rec:
    def __init__(self):
        self.call = None

    def __getattr__(self, name):
        def f(*a, **k):
            self.call = (name, a, k)
            return self
        return f


class Sched:
    ENG = ("pe", "act", "dve", "pool", "sp")

    def __init__(self, nc, stack):
        self.nc = nc
        self.stack = stack
        self.rec = {e: [] for e in self.ENG}
        self.esem = {e: Sem(stack.enter_context(nc.semaphore("s_" + e)), e) for e in self.ENG}
        self.waited = {e: {} for e in self.ENG}
        self.nsem = 5
        self.dsems = []
        self.off = (nc.sbuf_base + 63) // 64 * 64
        self.sb_cap = nc.sbuf_top
        self.nalloc = 0
        self.dres = {}

    def alloc(self, name, shape, dtype, dma=False):
        esz = 4 if dtype == F32 else 2
        n = 1
        for s in shape[1:]:
            n *= s
        nbytes = (n * esz + 63) // 64 * 64
        assert self.off + nbytes <= self.sb_cap, f"SBUF overflow at {name}: {self.off}+{nbytes}>{self.sb_cap}"
        self.nalloc += 1
        t = self.nc.alloc_sbuf_tensor_at(f"{name}_{self.nalloc}", list(shape), dtype, offset=self.off)
        self.off += nbytes
        ds = None
        if dma:
            if getattr(self, "depth", 0) > 0 and dma != "sw":
                ptr = getattr(self, "pool_ptr", 0)
                ds = self.dsem("pool%d" % ptr)
                self.pool_ptr = ptr + 1
            else:
                ds = self.dsem(name)
        return Buf(t.ap(), name, ds)

    def mark(self):
        self.depth = getattr(self, "depth", 0) + 1
        return (self.off, getattr(self, "pool_ptr", 0))

    def release(self, m):
        self.off, self.pool_ptr = m
        self.depth -= 1
        self.barrier()

    def barrier(self):
        deps = [(s, s.count) for s in self.dsems if s.count] + [(s, s.count) for s in self.esem.values() if s.count]
        for eng in self.ENG:
            wd = self.waited[eng]
            for s, v in deps:
                if wd.get(s, 0) < v:
                    wd[s] = v
                    self.rec[eng].append(("w", s.h, v))

    def dsem(self, name):
        if not hasattr(self, "_dsn"):
            self._dsn = {}
        if name in self._dsn:
            return self._dsn[name]
        s = self._dsn[name] = Sem(self.stack.enter_context(self.nc.semaphore(f"d{len(self.dsems)}_{name}")), name)
        self.dsems.append(s)
        return s

    def dr(self, key):
        r = self.dres.get(key)
        if r is None:
            r = self.dres[key] = Res(str(key))
        return r

    def _wait(self, eng, deps):
        wd = self.waited[eng]
        for s, v in deps:
            if eng == "pe" and s is self.esem["pe"]:
                continue
            if wd.get(s, 0) < v:
                wd[s] = v
                self.rec[eng].append(("w", s.h, v))

    def op(self, eng, fn, r=(), w=(), ds=None):
        deps = []
        for x in r:
            x = x.r if isinstance(x, Buf) else x
            if x.w is not None:
                deps.append(x.w)
        for x in w:
            x = x.r if isinstance(x, Buf) else x
            if x.w is not None and not (ds is not None and x.w[0] is ds):
                deps.append(x.w)
            deps.extend(x.rd.items())
        self._wait(eng, deps)
        if ds is None:
            s = self.esem[eng]
            s.count += 1
            inc = 1
        else:
            s = ds
            s.count += 16
            inc = 16
        ev = (s, s.count)
        p_ = _Rec()
        fn(p_)
        assert p_.call is not None
        import sys as _s
        fr = _s._getframe(1)
        while fr.f_code.co_name in ("op", "pe", "act", "dve", "pool", "dma", "headnorm", "rope"):
            fr = fr.f_back
        self.rec[eng].append(("i", p_.call, s.h, inc, fr.f_lineno))
        for x in r:
            x = x.r if isinstance(x, Buf) else x
            if x.rd.get(s, 0) < s.count:
                x.rd[s] = s.count
        for x in w:
            x = x.r if isinstance(x, Buf) else x
            x.w = ev
            x.rd = {}
        return ev

    def pe(self, fn, r=(), w=()):
        return self.op("pe", fn, r, w)

    def act(self, fn, r=(), w=()):
        return self.op("act", fn, r, w)

    def dve(self, fn, r=(), w=()):
        return self.op("dve", fn, r, w)

    def pool(self, fn, r=(), w=()):
        return self.op("pool", fn, r, w)

    def dma(self, q, out, in_, r=(), w=(), ds=None):
        assert ds is not None
        return self.op(q, lambda e: e.dma_start(out=out, in_=in_), r, w, ds)

    def wait_all(self, eng):
        deps = [(s, s.count) for s in self.dsems if s.count] + [(s, s.count) for s in self.esem.values() if s.count]
        self._wait(eng, [d for d in deps if d[0] is not self.esem[eng]])

    def replay(self):
        nc = self.nc
        hmap = {"pe": "tensor", "act": "scalar", "dve": "vector", "pool": "gpsimd", "sp": "sync"}
        with nc.Block() as block:
            for eng in self.ENG:
                items = self.rec[eng]

                def body(e, items=items):
                    for it in items:
                        if it[0] == "w":
                            e.wait_ge(it[1], it[2])
                        else:
                            nm, a_, k_ = it[1]
                            ins_ = getattr(e, nm)(*a_, **k_)
                            if len(it) > 4:
                                ins_.annotate("L%d" % it[4])
                            ins_.then_inc(it[2], it[3])

                getattr(block, hmap[eng])(body)


def make_consts():
    c = {}
    c["ident_bf"] = np.eye(128, dtype=np.float32).astype(ml_dtypes.bfloat16)
    c["ident_f"] = np.eye(128, dtype=np.float32)
    sel = np.zeros((2, 2, 128), np.float32)
    sel[0, 0, :] = 1.0
    sel[1, 1, :] = 1.0
    c["sel"] = sel
    return c


class Cfg:
    def __init__(self, T=4096, NCTX=256, depth=2, phases=None):
        self.T = T
        self.NCTX = NCTX
        self.NBC = NCTX // 128
        self.NBL = T // 128
        self.NB = self.NBC + self.NBL
        self.NTOK = self.NB * 128
        self.depth = depth
        self.phases = phases
        self.nlayers_dbg = depth
        self.dbg_src = "XA"


def build(cfg):
    from contextlib import ExitStack

    nc = bass.Bass("TRN2", target_bir_lowering=False)
    stack = ExitStack()
    with stack:
        K = Sched(nc, stack)
        _emit(nc, K, cfg)
        K.replay()
    return nc


def _emit(nc, K, cfg):
    NB, NBC, NBL, NTOK, T = cfg.NB, cfg.NBC, cfg.NBL, cfg.NTOK, cfg.T
    L = cfg.depth

    def din(name, shape, dt=F32):
        return nc.dram_tensor(name, list(shape), dt, kind="ExternalInput").ap()

    def dint(name, shape, dt=F32):
        return nc.dram_tensor(name, list(shape), dt, kind="Internal").ap()

    xin = din("xin", [NTOK, D])
    cloc = din("cloc", [2, D])
    ada_w = din("ada_w", [L, D, 9 * D])
    ada_b = din("ada_b", [L, 9 * D])
    norm_w = din("norm_w", [L, 3, D])
    ffn_w_in = din("ffn_w_in", [L, 2, D, 2 * DFF])
    ffn_w_out = din("ffn_w_out", [L, 2, DFF, D])
    ident_bf_d = din("ident_bf", [128, 128], BF16)
    ident_f_d = din("ident_f", [128, 128])
    sel_d = din("sel", [2, 2, 128])
    xout = nc.dram_tensor("xout", [T, D], F32, kind="ExternalOutput").ap()
    XA = dint("XA", [NTOK, D])
    XB = dint("XB", [NTOK, D])
    dbg = None
    if cfg.phases is not None:
        dbg = nc.dram_tensor("dbg", [NTOK, D], F32, kind="ExternalOutput").ap()

    PS = [Buf(nc.alloc_psum_tensor(f"ps{i}", [128, 512], F32).ap(), f"ps{i}") for i in range(8)]

    ident_bf = K.alloc("ident_bf", [128, 128], BF16, dma=True)
    ident_f = K.alloc("ident_f", [128, 128], F32, dma=True)
    sel = K.alloc("sel", [2, 2, 128], F32, dma=True)
    K.dma("sp", ident_bf[:, :], ident_bf_d, w=[ident_bf], ds=ident_bf.ds)
    K.dma("sp", ident_f[:, :], ident_f_d, w=[ident_f], ds=ident_f.ds)
    K.dma("sp", sel[:, :, :], sel_d.rearrange("k r p -> r k p"), w=[sel], ds=sel.ds)
    GR = dint("GR", [2, 3, D])
    modcol = K.alloc("modcol", [128, 48, 2], F32)
    gate = [K.alloc(f"gate{k}", [128, D], F32, dma=True) for k in range(2)]

    def kind_of(blk):
        return 1 if blk < NBC else 0

    def mod_phase(l):
        m0 = K.mark()
        modrow = K.alloc("modrow", [2, 9 * D], F32)
        craw = K.alloc("craw", [2, D], F32, dma=True)
        csil = K.alloc("csil", [2, D], F32)
        csilT = K.alloc("csilT", [128, 8, 2], F32)
        adab = K.alloc("adab", [2, 9 * D], F32, dma=True)
        nw = K.alloc("nw", [2, 3, D], F32, dma=True)
        abrow = K.alloc("abrow", [2, 3, D], F32)
        wa = [K.alloc(f"wa{i}", [128, 8, 512], F32, dma=True) for i in range(2)]
        K.dma("sp", craw[:, :], cloc, w=[craw], ds=craw.ds)
        for rr in range(2):
            K.dma("sp", adab[rr:rr + 1, :], ada_b[l:l + 1, :], w=[adab], ds=adab.ds)
            K.dma("sp", nw[rr:rr + 1, :, :], norm_w[l:l + 1, :, :], w=[nw], ds=nw.ds)
        K.act(lambda e: e.activation(out=csil[:, :], in_=craw[:, :], func=AF.Silu), r=[craw], w=[csil])
        pt = PS[0]
        for k in range(8):
            K.pe(lambda e, k=k: e.transpose(out=pt[:, k * 2:(k + 1) * 2], in_=csil[0:2, k * 128:(k + 1) * 128],
                                            identity=ident_f[0:2, 0:2]), r=[csil, ident_f], w=[pt])
        K.dve(lambda e: e.tensor_copy(out=csilT[:, :, :], in_=pt[:, 0:16].rearrange("p (k r) -> p k r", r=2)),
              r=[pt], w=[csilT])
        for cg in range(18):
            w_ = wa[cg % 2]
            K.dma("sp", w_[:, :, :], ada_w[l, :, cg * 512:(cg + 1) * 512].rearrange("(k p) n -> p k n", p=128),
                  w=[w_], ds=w_.ds)
            pm = PS[1 + cg % 2]
            for k in range(8):
                K.pe(lambda e, k=k, pm=pm, w_=w_: e.matmul(pm[0:2, :], lhsT=csilT[:, k, :], rhs=w_[:, k, :],
                                                          start=(k == 0), stop=(k == 7)),
                     r=[csilT, w_], w=[pm])
            K.dve(lambda e, pm=pm, cg=cg: e.tensor_tensor(out=modrow[:, cg * 512:(cg + 1) * 512], in0=pm[0:2, :],
                                                          in1=adab[:, cg * 512:(cg + 1) * 512], op=ALU.add),
                  r=[pm, adab], w=[modrow])
        for i in range(3):
            K.dve(lambda e, i=i: e.scalar_tensor_tensor(out=abrow[:, i, :], in0=modrow[:, (3 * i + 1) * D:(3 * i + 2) * D],
                                                        scalar=1.0, in1=nw[:, i, :], op0=ALU.add, op1=ALU.mult),
                  r=[modrow, nw], w=[abrow])
        pt = PS[3]
        for i in range(3):
            for which in range(2):
                for c in range(8):
                    idx = (i * 2 + which) * 8 + c
                    if which == 0:
                        src = abrow[0:2, i, c * 128:(c + 1) * 128]
                        rr = [abrow, ident_f]
                    else:
                        src = modrow[0:2, 3 * i * D + c * 128: 3 * i * D + (c + 1) * 128]
                        rr = [modrow, ident_f]
                    K.pe(lambda e, idx=idx, src=src: e.transpose(out=pt[:, idx * 2:(idx + 1) * 2], in_=src,
                                                                 identity=ident_f[0:2, 0:2]), r=rr, w=[pt])
        K.dve(lambda e: e.tensor_copy(out=modcol[:, :, :], in_=pt[:, 0:96].rearrange("p (a r) -> p a r", r=2)),
              r=[pt], w=[modcol])
        gds = K.dsem("grst")
        for jj in range(3):
            K.dma("sp", GR[:, jj, :], modrow[:, (3 * jj + 2) * D:(3 * jj + 3) * D], r=[modrow], w=[K.dr(("GR", jj))], ds=gds)
        K.release(m0)

    def load_gate(j, scale):
        for kind in range(2):
            g = gate[kind]
            K.dma("sp", g[:, :], GR[kind, j // 3:j // 3 + 1, :].broadcast_to([128, D]), r=[K.dr(("GR", j // 3))], w=[g], ds=g.ds)
            if scale != 1.0:
                K.dve(lambda e, g=g: e.tensor_scalar(out=g[:, :], in0=g[:, :], scalar1=scale, scalar2=None, op0=ALU.mult), r=[g], w=[g])

    def Acol(i, kind, c):
        return modcol[:, (i * 2 + 0) * 8 + c, kind:kind + 1]

    def Bcol(i, kind, c):
        return modcol[:, (i * 2 + 1) * 8 + c, kind:kind + 1]

    rstd_all = K.alloc("rstd_all", [128, NB], F32)
    ss_all = K.alloc("ss_all", [128, NB], F32)

    def stats_prepass(xsrc, blocks):
        m0 = K.mark()
        xs = [K.alloc(f"xs{i}", [128, D], F32, dma=True) for i in range(2)]
        jk = K.alloc("sjunk", [128, D], BF16)
        for n, blk in enumerate(blocks):
            b_ = xs[n % 2]
            K.dma("sp", b_[:, :], xsrc[blk * 128:(blk + 1) * 128, :], r=[K.dr((id(xsrc), blk))], w=[b_], ds=b_.ds)
            K.act(lambda e, b_=b_, blk=blk: e.activation(out=jk[:, :], in_=b_[:, :], func=AF.Square,
                                                        accum_out=ss_all[:, blk:blk + 1]), r=[b_], w=[jk, ss_all])
        K.act(lambda e: e.activation(out=rstd_all[:, :], in_=ss_all[:, :], func=AF.Ln, scale=1.0 / D, bias=EPS),
              r=[ss_all], w=[rstd_all])
        K.act(lambda e: e.activation(out=rstd_all[:, :], in_=rstd_all[:, :], func=AF.Exp, scale=-0.5),
              r=[rstd_all], w=[rstd_all])
        K.release(m0)

    def modnorm_T(xb, i, kind, hT_dst, tmp, blk):
        xn, pst = tmp
        K.dve(lambda e: e.tensor_scalar(out=xn[:, :], in0=xb[:, :], scalar1=rstd_all[:, blk:blk + 1], scalar2=None,
                                        op0=ALU.mult), r=[xb, rstd_all], w=[xn])
        pv = pst.ap.bitcast(BF16)
        for c in range(8):
            K.pe(lambda e, c=c: e.transpose(out=pv[:, c * 128:(c + 1) * 128], in_=xn[:, c * 128:(c + 1) * 128],
                                            identity=ident_bf[:, :]), r=[xn, ident_bf], w=[pst])
        for c in range(8):
            K.act(lambda e, c=c: e.activation(out=hT_dst(c), in_=pv[:, c * 128:(c + 1) * 128], func=AF.Identity,
                                              scale=Acol(i, kind, c), bias=Bcol(i, kind, c)),
                  r=[pst, modcol], w=[hT_dst.buf])

    def ffn_phase(l, s, i_norm, j_gate, xsrc, xdst, blocks, final_out=False):
        stats_prepass(xsrc, blocks)
        m0 = K.mark()
        load_gate(j_gate, 0.5)
        Win = K.alloc("Win", [128, 8, 2 * DFF], BF16, dma="sw")
        Wout = K.alloc("Wout", [128, 22, D], BF16, dma="sw")
        for k in range(8):
            K.dma("pool", Win[:, k, :], ffn_w_in[l, s, k * 128:(k + 1) * 128, :], w=[Win], ds=Win.ds)
        for k in range(22):
            K.dma("pool", Wout[:, k, :], ffn_w_out[l, s, k * 128:(k + 1) * 128, :], w=[Wout], ds=Wout.ds)
        xb = [K.alloc(f"xb{i}", [128, D], F32, dma=True) for i in range(2)]
        xr = [K.alloc(f"xr{i}", [128, D], F32, dma=True) for i in range(2)]
        xn = [K.alloc(f"xn{i}", [128, D], BF16) for i in range(2)]
        hT = [K.alloc(f"hT{i}", [128, 8, 256], BF16) for i in range(2)]
        aT = K.alloc("aT", [128, 22, 256], BF16)
        sg = [K.alloc(f"sg{i}", [128, 256], F32) for i in range(2)]
        tmp = [K.alloc("ytmp", [128, D], F32)] * 2
        tiles = [blocks[a:a + 2] for a in range(0, len(blocks), 2)]
        cnt = 0
        for ti, tile in enumerate(tiles):
            h = hT[ti % 2]
            for bi, blk in enumerate(tile):
                b_ = xb[cnt % 2]
                K.dma("sp", b_[:, :], xsrc[blk * 128:(blk + 1) * 128, :], r=[K.dr((id(xsrc), blk))], w=[b_], ds=b_.ds)

                def dst(c, h=h, bi=bi):
                    return h[:, c, bi * 128:(bi + 1) * 128]
                dst.buf = h
                modnorm_T(b_, i_norm, kind_of(blk), dst, (xn[cnt % 2], PS[0]), blk)
                cnt += 1
            nt = 128 * len(tile)
            for fc in range(22):
                pg = PS[1 + 2 * (fc % 2)]
                pu = PS[2 + 2 * (fc % 2)]
                for k in range(8):
                    K.pe(lambda e, k=k, pg=pg, fc=fc, h=h: e.matmul(pg[:, 0:nt], lhsT=Win[:, k, fc * 128:(fc + 1) * 128],
                                                                   rhs=h[:, k, 0:nt], start=(k == 0), stop=(k == 7)),
                         r=[Win, h], w=[pg])
                for k in range(8):
                    K.pe(lambda e, k=k, pu=pu, fc=fc, h=h: e.matmul(pu[:, 0:nt], lhsT=Win[:, k, DFF + fc * 128:DFF + (fc + 1) * 128],
                                                                   rhs=h[:, k, 0:nt], start=(k == 0), stop=(k == 7)),
                         r=[Win, h], w=[pu])
                s_ = sg[fc % 2]
                K.act(lambda e, pg=pg, s_=s_: e.activation(out=s_[:, 0:nt], in_=pg[:, 0:nt], func=AF.Silu),
                      r=[pg], w=[s_])
                K.dve(lambda e, pu=pu, s_=s_, fc=fc: e.tensor_tensor(out=aT[:, fc, 0:nt], in0=pu[:, 0:nt], in1=s_[:, 0:nt],
                                                                     op=ALU.mult), r=[pu, s_], w=[aT])
            for bi, blk in enumerate(tile):
                kind = kind_of(blk)
                t_ = tmp[bi % 2]
                for half in range(2):
                    py = PS[5 + half]
                    for fc in range(22):
                        K.pe(lambda e, fc=fc, py=py, bi=bi, half=half: e.matmul(
                            py[:, :], lhsT=aT[:, fc, bi * 128:(bi + 1) * 128], rhs=Wout[:, fc, half * 512:(half + 1) * 512],
                            start=(fc == 0), stop=(fc == 21)), r=[aT, Wout], w=[py])
                    K.dve(lambda e, py=py, half=half, t_=t_, kind=kind: e.tensor_tensor(
                        out=t_[:, half * 512:(half + 1) * 512], in0=py[:, :], in1=gate[kind][:, half * 512:(half + 1) * 512],
                        op=ALU.mult), r=[py, gate[kind]], w=[t_])
                r_ = xr[bi % 2]
                o_ = r_
                K.dma("sp", r_[:, :], xsrc[blk * 128:(blk + 1) * 128, :], r=[K.dr((id(xsrc), blk))], w=[r_], ds=r_.ds)
                K.pool(lambda e, r_=r_, o_=o_, t_=t_: e.tensor_tensor(out=o_[:, :], in0=r_[:, :], in1=t_[:, :], op=ALU.add),
                       r=[r_, t_], w=[o_])
                if final_out:
                    lb = blk - NBC
                    K.dma("sp", xout[lb * 128:(lb + 1) * 128, :], o_[:, :], r=[o_], w=[K.dr(("xout", lb))], ds=o_.ds)
                else:
                    K.dma("sp", xdst[blk * 128:(blk + 1) * 128, :], o_[:, :], r=[o_], w=[K.dr((id(xdst), blk))], ds=o_.ds)
        K.release(m0)

    NKB = NBC + 4 * NBL
    NCH = NB * 2
    GW = 2688
    mix_w_in = din("mix_w_in", [L, D, MIXC])
    mix_w_out = din("mix_w_out", [L, D, D])
    w_uq_d = din("mla_w_uq", [L, 256, 384])
    w_ukv_d = din("mla_w_ukv", [L, 128, 512])
    gains_d = din("gains", [L, 128, GW])
    lamrow_d = din("lamrow", [L, 128, 128])
    lbrow_d = din("lbrow", [128, 2, L, 512])
    rope_d = din("rope", [NTOK, 96])
    hmat_d = din("hmat", [2, 64, 4, 64])
    segm_d = din("segm", [128, 8])
    QT = dint("QT", [896, NTOK], BF16)
    KTL = dint("KTL", [768, T], BF16)
    KTC = dint("KTC", [768, NBC * 128], BF16)
    VL = dint("VL", [10 * 128, NBL * 65], BF16)
    VC = dint("VC", [10 * 128, NBC * 65], BF16)
    KTG = dint("KTG", [4 * 768, T], BF16)
    VG = dint("VG", [4 * 10 * 128, NBL * 65], BF16)
    ZH = dint("ZH", [NTOK, 2048])
    OH = dint("OH", [NTOK, 256])
    QGS = dint("QGS", [2, 4, 128, NTOK], BF16)
    HSL = dint("HSL", [128, 520])
    HSG = dint("HSG", [4 * 128, 520])
    OU = dint("OU", [NTOK, 1024])
    RG = [[0, 1, 2, 3], [4, 5, 6, 7]]
    ccsem = K.dsem("cc")
    ccsem2 = K.dsem("cc2")
    KPB = [0, 64, 128, 192, 256, 352, 448, 544, 640, 704, 768]

    def kt_row(r_, krow):
        for a, b in zip(KPB[:-1], KPB[1:]):
            if a <= krow < b:
                return 4 * a + r_ * (b - a) + (krow - a)
        raise AssertionError

    def allgather(src, dst, rkeys, wkeys, s=None):
        s = ccsem if s is None else s
        deps = []
        for k in rkeys:
            x = K.dr(k)
            if x.w is not None:
                deps.append(x.w)
        for k in wkeys:
            x = K.dr(k)
            if x.w is not None:
                deps.append(x.w)
            deps.extend(x.rd.items())
        K._wait("pool", deps)
        s.count += 1
        K.rec["pool"].append(("i", ("collective_compute", ("AllGather", ALU.bypass),
                                    dict(replica_groups=RG, ins=[src.opt()], outs=[dst.opt()])), s.h, 1))
        ev = (s, s.count)
        for k in wkeys:
            x = K.dr(k)
            x.w = ev
            x.rd = {}
        for k in rkeys:
            K.dr(k).rd[s] = s.count

    gains = K.alloc("gains", [128, GW], F32, dma=True)
    hmat = K.alloc("hmat", [64, 2, 4, 64], F32, dma=True)
    hmask = K.alloc("hmask", [64, 2, 64], BF16)
    segm = K.alloc("segm", [128, 8], F32, dma=True)
    neglam = K.alloc("neglam", [128, 1], F32)
    ones64 = K.alloc("ones64", [64, 1], F32)
    K.dma("sp", hmat[:, :, :, :], hmat_d.rearrange("d s m t -> s d m t"), w=[hmat], ds=hmat.ds)
    K.dma("sp", segm[:, :], segm_d, w=[segm], ds=segm.ds)
    K.dve(lambda e: e.memset(ones64[:, :], 1.0), w=[ones64])
    for d_ in range(2):
        K.dve(lambda e, d_=d_: e.tensor_copy(out=hmask[:, d_, :], in_=hmat[:, d_, 3, :]), r=[hmat], w=[hmask])

    G_AQK, G_BQ, G_BKV, G_BNQ, G_BRQ, G_BNK, G_BRK, G_D, G_SUB, G_HO = 0, 512, 768, 896, 1152, 1280, 1536, 1568, 1952, 2208

    def layer_consts(l):
        lam_init = 0.8 - 0.6 * float(np.exp(-0.3 * l))
        m0 = K.mark()
        K.dma("sp", gains[:, :], gains_d[l], w=[gains], ds=gains.ds)
        lr = K.alloc("lr", [128, 128], F32, dma=True)
        t1 = K.alloc("t1", [128, 2, 32], F32)
        s12 = K.alloc("s12", [128, 2], F32)
        K.dma("sp", lr[:, :], lamrow_d[l], w=[lr], ds=lr.ds)
        lv = lr.ap.rearrange("p (a b) -> p a b", b=32)
        K.dve(lambda e: e.tensor_tensor(out=t1[:, 0, :], in0=lv[:, 0, :], in1=lv[:, 1, :], op=ALU.mult), r=[lr], w=[t1])
        K.dve(lambda e: e.tensor_tensor(out=t1[:, 1, :], in0=lv[:, 2, :], in1=lv[:, 3, :], op=ALU.mult), r=[lr], w=[t1])
        K.dve(lambda e: e.tensor_reduce(out=s12[:, :], in_=t1[:, :, :], axis=AX.X, op=ALU.add), r=[t1], w=[s12])
        K.act(lambda e: e.activation(out=s12[:, :], in_=s12[:, :], func=AF.Exp), r=[s12], w=[s12])
        K.dve(lambda e: e.scalar_tensor_tensor(out=neglam[:, :], in0=s12[:, 1:2], scalar=-lam_init, in1=s12[:, 0:1],
                                               op0=ALU.add, op1=ALU.subtract), r=[s12], w=[neglam])
        K.dve(lambda e: e.tensor_scalar(out=gains[:, G_AQK:G_AQK + 256], in0=gains[:, G_AQK:G_AQK + 256], scalar1=32 ** -0.5,
                                        scalar2=None, op0=ALU.mult), r=[gains], w=[gains])
        K.dve(lambda e: e.tensor_scalar(out=gains[:, G_BNQ:G_BNQ + 384], in0=gains[:, G_BNQ:G_BNQ + 384], scalar1=96 ** -0.5,
                                        scalar2=None, op0=ALU.mult), r=[gains], w=[gains])
        K.dve(lambda e: e.tensor_scalar(out=gains[:, G_D:G_D + 256], in0=gains[:, G_D:G_D + 256], scalar1=64 ** -0.5,
                                        scalar2=None, op0=ALU.mult), r=[gains], w=[gains])
        K.dve(lambda e: e.tensor_scalar(out=gains[:, G_SUB:G_SUB + 256], in0=gains[:, G_SUB:G_SUB + 256], scalar1=1.0 - lam_init,
                                        scalar2=None, op0=ALU.mult), r=[gains], w=[gains])
        K.release(m0)

    def rsq(e_, ss, d):
        e_(lambda e: e.activation(out=ss, in_=ss, func=AF.Ln, scale=1.0 / d, bias=EPS))
        e_(lambda e: e.activation(out=ss, in_=ss, func=AF.Exp, scale=-0.5))

    def headnorm(src, H, d, gain, dst, sq, ss, ssbuf, eng, rr, ww):
        eng(lambda e: e.tensor_tensor(out=sq, in0=src, in1=src, op=ALU.mult), r=rr, w=[ww[1]])
        K.dve(lambda e: e.tensor_reduce(out=ss, in_=sq, axis=AX.X, op=ALU.add), r=[ww[1]], w=[ssbuf])
        K.act(lambda e: e.activation(out=ss, in_=ss, func=AF.Ln, scale=1.0 / d, bias=EPS), r=[ssbuf], w=[ssbuf])
        K.act(lambda e: e.activation(out=ss, in_=ss, func=AF.Exp, scale=-0.5), r=[ssbuf], w=[ssbuf])
        np_ = ss.shape[0]
        eng(lambda e: e.tensor_tensor(out=sq, in0=src, in1=ss.unsqueeze(2).broadcast_to([np_, H, d]), op=ALU.mult),
            r=rr + [ssbuf], w=[ww[1]])
        eng(lambda e: e.tensor_tensor(out=dst, in0=sq, in1=gain, op=ALU.mult), r=[ww[1], gains], w=[ww[0]])

    ROPE = [None]

    def rope(src, H, hd, cos, sin, dst, ta, tb, eng, rr, ww, tw):
        ropeT = ROPE[0]
        cb = cos.unsqueeze(1).broadcast_to([128, H, hd])
        sb_ = sin.unsqueeze(1).broadcast_to([128, H, hd])
        x1, x2 = src[:, :, 0, :], src[:, :, 1, :]
        eng(lambda e: e.tensor_tensor(out=ta, in0=x1, in1=cb, op=ALU.mult), r=rr + [ropeT], w=[tw[0]])
        eng(lambda e: e.tensor_tensor(out=tb, in0=x2, in1=sb_, op=ALU.mult), r=rr + [ropeT], w=[tw[1]])
        eng(lambda e: e.tensor_tensor(out=dst[:, :, 0, :], in0=ta, in1=tb, op=ALU.subtract), r=[tw[0], tw[1]], w=ww)
        eng(lambda e: e.tensor_tensor(out=ta, in0=x1, in1=sb_, op=ALU.mult), r=rr + [ropeT], w=[tw[0]])
        eng(lambda e: e.tensor_tensor(out=tb, in0=x2, in1=cb, op=ALU.mult), r=rr + [ropeT], w=[tw[1]])
        eng(lambda e: e.tensor_tensor(out=dst[:, :, 1, :], in0=ta, in1=tb, op=ALU.add), r=[tw[0], tw[1]], w=ww)

    def mixin_phase(l, xsrc, blocks):
        stats_prepass(xsrc, blocks)
        m0 = K.mark()
        ropeT = K.alloc("ropeT", [128, NB, 96], F32, dma=True)
        for n0 in range(0, NB, 8):
            n1 = min(NB, n0 + 8)
            K.dma("sp", ropeT[:, n0:n1, :], rope_d[n0 * 128:n1 * 128, :].rearrange("(n p) c -> p n c", p=128), w=[ropeT], ds=ropeT.ds)
        ROPE[0] = ropeT
        Wm = K.alloc("Wm", [128, 8, MIXC], BF16, dma="sw")
        wuq = K.alloc("wuq", [128, 2, 384], BF16, dma="sw")
        wukv = K.alloc("wukv", [128, 512], BF16, dma="sw")
        for k in range(8):
            K.dma("pool", Wm[:, k, :], mix_w_in[l, k * 128:(k + 1) * 128, :], w=[Wm], ds=Wm.ds)
        for k in range(2):
            K.dma("pool", wuq[:, k, :], w_uq_d[l, k * 128:(k + 1) * 128, :], w=[wuq], ds=wuq.ds)
        K.dma("pool", wukv[:, :], w_ukv_d[l], w=[wukv], ds=wukv.ds)
        xb = [K.alloc(f"mxb{i}", [128, D], F32, dma=True) for i in range(2)]
        xn = K.alloc("mxn", [128, D], BF16)
        hT = [K.alloc(f"mhT{i}", [128, 8, 128], BF16) for i in range(2)]
        zs = [K.alloc(f"zs{i}", [128, MIXC], F32, dma=True) for i in range(2)]
        sq = K.alloc("sq", [128, 1024], F32)
        wk = K.alloc("wk", [128, 1024], F32)
        ta = K.alloc("ta", [128, 512], F32)
        tb = K.alloc("tb", [128, 512], F32)
        ssb = K.alloc("ssb", [128, 32], F32)
        qa = K.alloc("qa", [128, 512], BF16)
        qd = K.alloc("qd", [128, 384], BF16)
        lat = K.alloc("lat", [128, 384], BF16)
        latT = K.alloc("latT", [128, 3, 128], BF16)
        bq = K.alloc("bq", [128, 384], F32)
        bkv = K.alloc("bkv", [128, 512], F32)
        bqo = K.alloc("bqo", [128, 4, 96], BF16)
        bko = K.alloc("bko", [128, 4, 96], BF16)
        krr = K.alloc("krr", [128, 32], F32)
        qst = [K.alloc(f"qst{i}", [128, 7, 128], BF16, dma=True) for i in range(2)]
        kst = [K.alloc(f"kst{i}", [128, 6, 128], BF16, dma=True) for i in range(2)]
        vst = [K.alloc(f"vst{i}", [128, 10, 65], BF16, dma=True) for i in range(2)]
        for v_ in vst:
            K.dve(lambda e, v_=v_: e.memset(v_[:, :, :], 1.0), w=[v_])
        groups = [(0, 512), (512, 768), (768, 1184), (1184, 1696), (1696, 2208), (2208, 2720), (2720, 3232), (3232, 3744)]
        for n, blk in enumerate(blocks):
            kind = kind_of(blk)
            b_ = xb[n % 2]
            h = hT[n % 2]
            z = zs[n % 2]
            K.dma("sp", b_[:, :], xsrc[blk * 128:(blk + 1) * 128, :], r=[K.dr((id(xsrc), blk))], w=[b_], ds=b_.ds)

            def dst(c, h=h):
                return h[:, c, :]
            dst.buf = h
            modnorm_T(b_, 1, kind, dst, (xn, PS[0]), blk)
            for gi, (c0, c1) in enumerate(groups):
                pz = PS[1 + gi % 3]
                for k in range(8):
                    K.pe(lambda e, k=k, pz=pz, c0=c0, c1=c1, h=h: e.matmul(pz[:, 0:c1 - c0], lhsT=h[:, k, :], rhs=Wm[:, k, c0:c1],
                                                                         start=(k == 0), stop=(k == 7)), r=[h, Wm], w=[pz])
                if gi % 2 == 0:
                    K.act(lambda e, pz=pz, c0=c0, c1=c1, z=z: e.activation(out=z[:, c0:c1], in_=pz[:, 0:c1 - c0], func=AF.Copy),
                          r=[pz], w=[z])
                else:
                    K.dve(lambda e, pz=pz, c0=c0, c1=c1, z=z: e.tensor_copy(out=z[:, c0:c1], in_=pz[:, 0:c1 - c0]), r=[pz], w=[z])
            K.dma("sp", ZH[blk * 128:(blk + 1) * 128, :], z[:, 1184:3232], r=[z], w=[K.dr(("ZH", blk))], ds=z.ds)
            cosA, sinA = ropeT[:, blk, 0:16], ropeT[:, blk, 16:32]
            cosD, sinD = ropeT[:, blk, 32:64], ropeT[:, blk, 64:96]
            q_ = qst[n % 2]
            k_ = kst[n % 2]
            v_ = vst[n % 2]
            headnorm(z[:, 0:512].rearrange("p (h d) -> p h d", d=32), 16, 32,
                     gains[:, G_AQK:G_AQK + 512].rearrange("p (h d) -> p h d", d=32),
                     wk[:, 0:512].rearrange("p (h d) -> p h d", d=32), sq[:, 0:512].rearrange("p (h d) -> p h d", d=32),
                     ssb[:, 0:16], ssb, K.dve, [z], [wk, sq])
            rope(wk[:, 0:512].rearrange("p (h two d) -> p h two d", two=2, d=16), 16, 16, cosA, sinA,
                 qa[:, :].rearrange("p (h two d) -> p h two d", two=2, d=16),
                 ta[:, 0:256].rearrange("p (h d) -> p h d", d=16), tb[:, 0:256].rearrange("p (h d) -> p h d", d=16),
                 K.dve, [wk], [qa], [ta, tb])
            headnorm(z[:, 3232:3616].rearrange("p (h d) -> p h d", d=64), 6, 64,
                     gains[:, G_D:G_D + 384].rearrange("p (h d) -> p h d", d=64),
                     wk[:, 512:896].rearrange("p (h d) -> p h d", d=64), sq[:, 512:896].rearrange("p (h d) -> p h d", d=64),
                     ssb[:, 16:22], ssb, K.pool, [z], [wk, sq])
            rope(wk[:, 512:896].rearrange("p (h two d) -> p h two d", two=2, d=32), 6, 32, cosD, sinD,
                 qd[:, :].rearrange("p (h two d) -> p h two d", two=2, d=32),
                 ta[:, 256:448].rearrange("p (h d) -> p h d", d=32), tb[:, 256:448].rearrange("p (h d) -> p h d", d=32),
                 K.pool, [wk], [qd], [ta, tb])
            headnorm(z[:, 768:1024].rearrange("p (h d) -> p h d", d=256), 1, 256,
                     gains[:, G_BQ:G_BQ + 256].rearrange("p (h d) -> p h d", d=256),
                     lat[:, 0:256].rearrange("p (h d) -> p h d", d=256), sq[:, 0:256].rearrange("p (h d) -> p h d", d=256),
                     ssb[:, 22:23], ssb, K.dve, [z], [lat, sq])
            headnorm(z[:, 1024:1152].rearrange("p (h d) -> p h d", d=128), 1, 128,
                     gains[:, G_BKV:G_BKV + 128].rearrange("p (h d) -> p h d", d=128),
                     lat[:, 256:384].rearrange("p (h d) -> p h d", d=128), sq[:, 256:384].rearrange("p (h d) -> p h d", d=128),
                     ssb[:, 23:24], ssb, K.dve, [z], [lat, sq])
            pt = PS[4]
            ptv = pt.ap.bitcast(BF16)
            for c in range(3):
                K.pe(lambda e, c=c: e.transpose(out=ptv[:, c * 128:(c + 1) * 128], in_=lat[:, c * 128:(c + 1) * 128],
                                                identity=ident_bf[:, :]), r=[lat, ident_bf], w=[pt])
            K.act(lambda e: e.activation(out=latT[:, :, :], in_=ptv[:, 0:384].rearrange("p (c t) -> p c t", t=128), func=AF.Copy),
                  r=[pt], w=[latT])
            pq = PS[5]
            for c in range(2):
                K.pe(lambda e, c=c: e.matmul(pq[:, 0:384], lhsT=latT[:, c, :], rhs=wuq[:, c, :], start=(c == 0), stop=(c == 1)),
                     r=[latT, wuq], w=[pq])
            K.act(lambda e: e.activation(out=bq[:, :], in_=pq[:, 0:384], func=AF.Copy), r=[pq], w=[bq])
            pk = PS[6]
            K.pe(lambda e: e.matmul(pk[:, :], lhsT=latT[:, 2, :], rhs=wukv[:, :], start=True, stop=True), r=[latT, wukv], w=[pk])
            K.act(lambda e: e.activation(out=bkv[:, :], in_=pk[:, :], func=AF.Copy), r=[pk], w=[bkv])
            bq3 = bq[:, :].rearrange("p (h d) -> p h d", d=96)
            bkv3 = bkv[:, :].rearrange("p (h d) -> p h d", d=128)
            headnorm(bq3[:, :, 0:64], 4, 64, gains[:, G_BNQ:G_BNQ + 256].rearrange("p (h d) -> p h d", d=64),
                     bqo[:, :, 0:64], sq[:, 0:256].rearrange("p (h d) -> p h d", d=64), ssb[:, 24:28], ssb, K.dve, [bq], [bqo, sq])
            headnorm(bq3[:, :, 64:96], 4, 32, gains[:, G_BRQ:G_BRQ + 128].rearrange("p (h d) -> p h d", d=32),
                     wk[:, 896:1024].rearrange("p (h d) -> p h d", d=32), sq[:, 256:384].rearrange("p (h d) -> p h d", d=32),
                     ssb[:, 28:32], ssb, K.dve, [bq], [wk, sq])
            rope(wk[:, 896:1024].rearrange("p (h two d) -> p h two d", two=2, d=16), 4, 16, cosA, sinA,
                 bqo[:, :, 64:96].rearrange("p h (two d) -> p h two d", two=2),
                 ta[:, 448:512].rearrange("p (h d) -> p h d", d=16), tb[:, 448:512].rearrange("p (h d) -> p h d", d=16),
                 K.dve, [wk], [bqo], [ta, tb])
            headnorm(bkv3[:, :, 0:64], 4, 64, gains[:, G_BNK:G_BNK + 256].rearrange("p (h d) -> p h d", d=64),
                     bko[:, :, 0:64], sq[:, 384:640].rearrange("p (h d) -> p h d", d=64), ssb[:, 24:28], ssb, K.pool, [bkv], [bko, sq])
            headnorm(z[:, 1152:1184].rearrange("p (h d) -> p h d", d=32), 1, 32,
                     gains[:, G_BRK:G_BRK + 32].rearrange("p (h d) -> p h d", d=32),
                     krr[:, :].rearrange("p (h d) -> p h d", d=32), sq[:, 640:672].rearrange("p (h d) -> p h d", d=32),
                     ssb[:, 22:23], ssb, K.pool, [z], [krr, sq])
            rope(krr[:, :].rearrange("p (h two d) -> p h two d", two=2, d=16), 1, 16, cosA, sinA,
                 bko[:, 0:1, 64:96].rearrange("p h (two d) -> p h two d", two=2),
                 ta[:, 448:464].rearrange("p (h d) -> p h d", d=16), tb[:, 448:464].rearrange("p (h d) -> p h d", d=16),
                 K.pool, [krr], [bko], [ta, tb])
            for hh in range(1, 4):
                K.pool(lambda e, hh=hh: e.tensor_copy(out=bko[:, hh, 64:96], in_=bko[:, 0, 64:96]), r=[bko], w=[bko])
            K.act(lambda e, v_=v_, z=z: e.activation(out=v_[:, 0:4, 0:64], in_=z[:, 512:768].rearrange("p (h d) -> p h d", d=64),
                                                     func=AF.Copy), r=[z], w=[v_])
            K.act(lambda e, v_=v_: e.activation(out=v_[:, 4:8, 0:64], in_=bkv3[:, :, 64:128], func=AF.Copy), r=[bkv], w=[v_])
            K.act(lambda e, v_=v_, z=z: e.activation(out=v_[:, 8:10, 0:64], in_=z[:, 3616:3744].rearrange("p (h d) -> p h d", d=64),
                                                     func=AF.Copy), r=[z], w=[v_])
            pa = PS[7]
            pav = pa.ap.bitcast(BF16)
            srcs_q = [(qa, 0), (qa, 128), (bqo, 0), (bqo, 128), (bqo, 256), (qd, 0), (qd, 128)]
            bqo2 = bqo[:, :, :].rearrange("p h d -> p (h d)")
            bko2 = bko[:, :, :].rearrange("p h d -> p (h d)")
            for c, (sb_, off) in enumerate(srcs_q):
                sv = bqo2 if sb_ is bqo else sb_[:, :]
                K.pe(lambda e, c=c, sv=sv, off=off: e.transpose(out=pav[:, c * 128:(c + 1) * 128], in_=sv[:, off:off + 128],
                                                                identity=ident_bf[:, :]), r=[sb_, ident_bf], w=[pa])
            K.act(lambda e, q_=q_: e.activation(out=q_[:, :, :], in_=pav[:, 0:896].rearrange("p (c t) -> p c t", t=128), func=AF.Copy),
                  r=[pa], w=[q_])
            pb = PS[4]
            pbv = pb.ap.bitcast(BF16)
            srcs_k = [(qa, 256), (qa, 384), (bko, 0), (bko, 128), (bko, 256), (qd, 256)]
            for c, (sb_, off) in enumerate(srcs_k):
                sv = bko2 if sb_ is bko else sb_[:, :]
                K.pe(lambda e, c=c, sv=sv, off=off: e.transpose(out=pbv[:, c * 128:(c + 1) * 128], in_=sv[:, off:off + 128],
                                                                identity=ident_bf[:, :]), r=[sb_, ident_bf], w=[pb])
            K.dve(lambda e, k_=k_: e.tensor_copy(out=k_[:, :, :], in_=pbv[:, 0:768].rearrange("p (c t) -> p c t", t=128)),
                  r=[pb], w=[k_])
            K.dma("sp", QT[:, blk * 128:(blk + 1) * 128].rearrange("(c p) t -> p c t", p=128), q_[:, :, :], r=[q_],
                  w=[K.dr(("QT", blk))], ds=q_.ds)
            if kind == 1:
                K.dma("sp", KTC[:, blk * 128:(blk + 1) * 128].rearrange("(c p) t -> p c t", p=128), k_[:, :, :], r=[k_],
                      w=[K.dr("KTC")], ds=k_.ds)
                K.dma("sp", VC[:, blk * 65:(blk + 1) * 65].rearrange("(h p) c -> p h c", p=128), v_[:, :, :], r=[v_],
                      w=[K.dr("VC")], ds=v_.ds)
            else:
                lbk = blk - NBC
                K.dma("sp", KTL[:, lbk * 128:(lbk + 1) * 128].rearrange("(c p) t -> p c t", p=128), k_[:, :, :], r=[k_],
                      w=[K.dr("KTL")], ds=k_.ds)
                K.dma("sp", VL[:, lbk * 65:(lbk + 1) * 65].rearrange("(h p) c -> p h c", p=128), v_[:, :, :], r=[v_],
                      w=[K.dr("VL")], ds=v_.ds)
        K.release(m0)
        for a, b in zip(KPB[:-1], KPB[1:]):
            allgather(KTL[a:b, :], KTG[4 * a:4 * b, :], ["KTL"], ["KTG"])
        for hv in range(10):
            allgather(VL[hv * 128:(hv + 1) * 128, :], VG[hv * 512:(hv + 1) * 512, :], ["VL"], ["VG"])
        for k_ in ("KTG", "VG"):
            K.dr(k_).w = (ccsem, ccsem.count)

    def hgrn_phase(l):
        m0 = K.mark()
        OHs = K.alloc("OHs", [64, NCH, 256], F32, dma=True)
        lbt = K.alloc("lbt", [128, 2, 2, 512], F32)
        lb_in = K.alloc("lb_in", [128, 2, L, 512], F32, dma=True)
        K.dma("sp", lb_in[:, :, :, :], lbrow_d, w=[lb_in], ds=lb_in.ds)
        if l == 0:
            K.dve(lambda e: e.memset(lbt[:, :, 0, :], 0.0), w=[lbt])
            K.dve(lambda e: e.memset(lbt[:, :, 1, :], 1.0), w=[lbt])
        else:
            assert L == 2 and l == 1
            K.dve(lambda e: e.tensor_tensor(out=lbt[:, :, 0, :], in0=lb_in[:, :, 1, :], in1=lb_in[:, :, 0, :], op=ALU.subtract),
                  r=[lb_in], w=[lbt])
            K.act(lambda e: e.activation(out=lbt[:, :, 0, :], in_=lbt[:, :, 0, :], func=AF.Sigmoid), r=[lbt], w=[lbt])
            K.dve(lambda e: e.tensor_scalar(out=lbt[:, :, 1, :], in0=lbt[:, :, 0, :], scalar1=-1.0, scalar2=1.0,
                                            op0=ALU.mult, op1=ALU.add), r=[lbt], w=[lbt])
        Sst = K.alloc("Sst", [128, 4, 64], F32)
        Sbf = K.alloc("Sbf", [128, 4, 64], BF16)
        Plog = K.alloc("Plog", [128, 4], F32)
        ePl = K.alloc("ePl", [128, 4], F32)
        dec = K.alloc("dec", [128, 4], F32)
        HS = K.alloc("HS", [128, 2, 260], F32, dma=True)
        Sctx = K.alloc("Sctx", [128, 2, 256], F32)
        zq = [K.alloc(f"zq{i}", [64, 512], F32, dma=True) for i in range(2)]
        zx = [K.alloc(f"zx{i}", [64, 512], F32, dma=True) for i in range(2)]
        zv = [K.alloc(f"zv{i}", [64, 256], F32, dma=True) for i in range(2)]
        sg = K.alloc("hsg", [64, 512], F32)
        qs = K.alloc("hqs", [64, 512], F32)
        ff = K.alloc("hff", [64, 512], F32)
        logf = K.alloc("hlogf", [64, 512], F32)
        kk = K.alloc("hkk", [64, 512], F32)
        ex = [K.alloc(f"hex{i}", [64, 512], F32) for i in range(4)]
        ops_ = [K.alloc(f"hop{i}", [64, 512], BF16) for i in range(4)]
        vb = K.alloc("hvb", [64, 256], BF16)
        TTs = K.alloc("hTT", [128, 12, 64], BF16)
        d1c = K.alloc("hd1c", [64, 512], F32)
        attL = K.alloc("hattL", [32, 4, 64], BF16)
        attH = K.alloc("hattH", [32, 4, 64], BF16)
        zvH = [K.alloc(f"zvH{i}", [32, 256], F32, dma=True) for i in range(2)]
        vbH = K.alloc("hvbH", [32, 256], BF16)
        mk = K.alloc("hmk", [32, 2, 2, 64], BF16)
        mkf = K.alloc("hmkf", [32, 2, 2, 64], F32, dma=True)
        for d2 in range(2):
            for hf in range(2):
                K.dma("sp", mkf[:, d2, hf, :], hmat_d[d2, hf * 32:(hf + 1) * 32, 3, :], w=[mkf], ds=mkf.ds)
        K.dve(lambda e: e.tensor_copy(out=mk[:, :, :, :], in_=mkf[:, :, :, :]), r=[mkf], w=[mk])
        qgs = [K.alloc(f"hqgs{i}", [128, 4, 64], BF16, dma=True) for i in range(2)]
        cnt = 0
        for d_ in range(2):
            K.dve(lambda e: e.memset(Sst[:, :, :], 0.0), w=[Sst])
            K.dve(lambda e: e.memset(Sbf[:, :, :], 0.0), w=[Sbf])
            order = list(range(NCH)) if d_ == 0 else (list(range(2 * NBC - 1, -1, -1)) + list(range(NCH - 1, 2 * NBC - 1, -1)))
            xoff = 512 if d_ == 0 else 1024
            for ci, c in enumerate(order):
                if ci == 2 * NBC:
                    K.dve(lambda e, d_=d_: e.tensor_copy(out=Sctx[:, d_, :], in_=Sst[:, :, :].rearrange("p h v -> p (h v)")),
                          r=[Sst], w=[Sctx])
                    K.dve(lambda e: e.memset(Sst[:, :, :], 0.0), w=[Sst])
                    K.dve(lambda e: e.memset(Sbf[:, :, :], 0.0), w=[Sbf])
                    K.dve(lambda e: e.memset(Plog[:, :], 0.0), w=[Plog])
                    K.dve(lambda e: e.memset(ePl[:, :], 1.0), w=[ePl])
                latent = ci >= 2 * NBC
                q_, x_, v_ = zq[cnt % 2], zx[cnt % 2], zv[cnt % 2]
                cnt += 1
                rows = slice(c * 64, (c + 1) * 64)
                zr = K.dr(("ZH", c // 2))
                K.dma("sp", q_[:, :], ZH[rows, 0:512], r=[zr], w=[q_], ds=q_.ds)
                K.dma("sp", x_[:, :], ZH[rows, xoff:xoff + 512], r=[zr], w=[x_], ds=x_.ds)
                K.dma("sp", v_[:, :], ZH[rows, 1536:1792], r=[zr], w=[v_], ds=v_.ds)
                vh_ = zvH[cnt % 2]
                K.dma("sp", vh_[:, :], ZH[c * 64 + 32:(c + 1) * 64, 1536:1792], r=[zr], w=[vh_], ds=vh_.ds)
                K.pool(lambda e, vh_=vh_: e.tensor_copy(out=vbH[:, :], in_=vh_[:, :]), r=[vh_], w=[vbH])
                K.act(lambda e, q_=q_: e.activation(out=sg[:, :], in_=q_[:, :], func=AF.Sigmoid), r=[q_], w=[sg])
                K.dve(lambda e, q_=q_: e.tensor_tensor(out=qs[:, :], in0=q_[:, :], in1=sg[:, :], op=ALU.mult), r=[q_, sg], w=[qs])
                K.act(lambda e, x_=x_: e.activation(out=ff[:, :], in_=x_[:, :], func=AF.Sigmoid), r=[x_], w=[ff])
                K.pool(lambda e, d_=d_: e.tensor_tensor(out=ff[:, :], in0=ff[:, :], in1=lbt[0:64, d_, 1, :], op=ALU.mult), r=[ff, lbt], w=[ff])
                K.pool(lambda e, d_=d_: e.tensor_tensor(out=ff[:, :], in0=ff[:, :], in1=lbt[0:64, d_, 0, :], op=ALU.add), r=[ff, lbt], w=[ff])
                K.act(lambda e: e.activation(out=logf[:, :], in_=ff[:, :], func=AF.Ln), r=[ff], w=[logf])
                K.pool(lambda e: e.tensor_scalar(out=kk[:, :], in0=ff[:, :], scalar1=-1.0, scalar2=1.0, op0=ALU.mult, op1=ALU.add),
                       r=[ff], w=[kk])
                K.pool(lambda e, v_=v_: e.tensor_copy(out=vb[:, :], in_=v_[:, :]), r=[v_], w=[vb])
                pd = [PS[0], PS[1], PS[2]]
                for m in range(3):
                    K.pe(lambda e, m=m, d_=d_: e.matmul(pd[m][0:64, :], lhsT=hmat[:, d_, m, :], rhs=logf[:, :], start=True, stop=True),
                         r=[hmat, logf], w=[pd[m]])
                K.dve(lambda e: e.tensor_scalar(out=d1c[:, :], in0=pd[0][0:64, :], scalar1=-80.0, scalar2=80.0, op0=ALU.max, op1=ALU.min),
                      r=[pd[0]], w=[d1c])
                K.act(lambda e: e.activation(out=ex[0][:, :], in_=d1c[:, :], func=AF.Exp), r=[d1c], w=[ex[0]])
                K.act(lambda e: e.activation(out=ex[1][:, :], in_=d1c[:, :], func=AF.Exp, scale=-1.0), r=[d1c], w=[ex[1]])
                K.act(lambda e: e.activation(out=ex[2][:, :], in_=pd[2][0:64, :], func=AF.Exp), r=[pd[2]], w=[ex[2]])
                K.act(lambda e: e.activation(out=ex[3][:, :], in_=pd[1][0:64, :], func=AF.Exp), r=[pd[1]], w=[ex[3]])
                K.dve(lambda e: e.tensor_tensor(out=ops_[0][:, :], in0=qs[:, :], in1=ex[0][:, :], op=ALU.mult), r=[qs, ex[0]], w=[ops_[0]])
                K.dve(lambda e: e.tensor_tensor(out=ops_[1][:, :], in0=kk[:, :], in1=ex[1][:, :], op=ALU.mult), r=[kk, ex[1]], w=[ops_[1]])
                K.dve(lambda e: e.tensor_tensor(out=ops_[2][:, :], in0=qs[:, :], in1=ex[2][:, :], op=ALU.mult), r=[qs, ex[2]], w=[ops_[2]])
                K.pool(lambda e: e.tensor_tensor(out=ops_[3][:, :], in0=kk[:, :], in1=ex[3][:, :], op=ALU.mult), r=[kk, ex[3]], w=[ops_[3]])
                ptt = PS[3]
                pttv = ptt.ap.bitcast(BF16)
                for a in range(3):
                    for h in range(4):
                        j = a * 4 + h
                        K.pe(lambda e, a=a, h=h, j=j: e.transpose(out=pttv[:, j * 64:(j + 1) * 64], in_=ops_[a][:, h * 128:(h + 1) * 128],
                                                                  identity=ident_bf[0:64, 0:64]), r=[ops_[a], ident_bf], w=[ptt])
                K.dve(lambda e: e.tensor_copy(out=TTs[:, :, :], in_=pttv[:, 0:768].rearrange("p (j t) -> p j t", t=64)), r=[ptt], w=[TTs])
                pat = PS[4]
                if ci == 0:
                    K.dve(lambda e: e.memset(attL[:, :, :], 0.0), w=[attL])
                    K.dve(lambda e: e.memset(attH[:, :, :], 0.0), w=[attH])
                for h in range(4):
                    if d_ == 0:
                        K.pe(lambda e, h=h: e.matmul(pat[0:32, h * 64:(h + 1) * 64], lhsT=TTs[:, 4 + h, 0:32], rhs=TTs[:, h, 0:64], start=True, stop=True),
                             r=[TTs], w=[pat])
                        K.pe(lambda e, h=h: e.matmul(pat[0:32, 256 + h * 64 + 32:256 + (h + 1) * 64], lhsT=TTs[:, 4 + h, 32:64], rhs=TTs[:, h, 32:64],
                                                     start=True, stop=True), r=[TTs], w=[pat])
                    else:
                        K.pe(lambda e, h=h: e.matmul(pat[0:32, 256 + h * 64:256 + (h + 1) * 64], lhsT=TTs[:, 4 + h, 32:64], rhs=TTs[:, h, 0:64],
                                                     start=True, stop=True), r=[TTs], w=[pat])
                        K.pe(lambda e, h=h: e.matmul(pat[0:32, h * 64:h * 64 + 32], lhsT=TTs[:, 4 + h, 0:32], rhs=TTs[:, h, 0:32],
                                                     start=True, stop=True), r=[TTs], w=[pat])
                pL = pat[0:32, 0:256].rearrange("p (h t) -> p h t", t=64)
                pH = pat[0:32, 256:512].rearrange("p (h t) -> p h t", t=64)
                if d_ == 0:
                    K.dve(lambda e, d_=d_: e.tensor_tensor(out=attL[:, :, :], in0=pL, in1=mk[:, d_, 0, :].unsqueeze(1).broadcast_to([32, 4, 64]),
                                                           op=ALU.mult), r=[pat, mk], w=[attL])
                    K.dve(lambda e, d_=d_: e.tensor_tensor(out=attH[:, :, 32:64], in0=pH[:, :, 32:64],
                                                           in1=mk[:, d_, 1, 32:64].unsqueeze(1).broadcast_to([32, 4, 32]), op=ALU.mult),
                          r=[pat, mk], w=[attH])
                else:
                    K.dve(lambda e, d_=d_: e.tensor_tensor(out=attH[:, :, :], in0=pH, in1=mk[:, d_, 1, :].unsqueeze(1).broadcast_to([32, 4, 64]),
                                                           op=ALU.mult), r=[pat, mk], w=[attH])
                    K.dve(lambda e, d_=d_: e.tensor_tensor(out=attL[:, :, 0:32], in0=pL[:, :, 0:32],
                                                           in1=mk[:, d_, 0, 0:32].unsqueeze(1).broadcast_to([32, 4, 32]), op=ALU.mult),
                          r=[pat, mk], w=[attL])
                po = PS[5]
                for h in range(4):
                    K.pe(lambda e, h=h: e.matmul(po[0:64, h * 64:(h + 1) * 64], lhsT=attL[:, h, :], rhs=vb[0:32, h * 64:(h + 1) * 64],
                                                 start=True, stop=False), r=[attL, vb], w=[po])
                    K.pe(lambda e, h=h: e.matmul(po[0:64, h * 64:(h + 1) * 64], lhsT=attH[:, h, :], rhs=vbH[:, h * 64:(h + 1) * 64],
                                                 start=False, stop=False), r=[attH, vbH], w=[po])
                    K.pe(lambda e, h=h: e.matmul(po[0:64, h * 64:(h + 1) * 64], lhsT=TTs[:, 8 + h, :], rhs=Sbf[:, h, :],
                                                 start=False, stop=True), r=[TTs, Sbf], w=[po])
                if d_ == 0:
                    K.act(lambda e, c=c: e.activation(out=OHs[:, c, :], in_=po[0:64, 0:256], func=AF.Copy), r=[po], w=[OHs])
                else:
                    K.dve(lambda e, c=c: e.tensor_tensor(out=OHs[:, c, :], in0=OHs[:, c, :], in1=po[0:64, 0:256], op=ALU.add),
                          r=[po, OHs], w=[OHs])
                if latent:
                    g_ = qgs[cnt % 2]
                    for h in range(4):
                        K.act(lambda e, h=h, g_=g_: e.activation(out=g_[:, h, :], in_=TTs[:, 8 + h, :], func=AF.Copy, scale=ePl[:, h:h + 1]),
                              r=[TTs, ePl], w=[g_])
                    K.dma("sp", QGS[d_, :, :, c * 64:(c + 1) * 64].rearrange("h k t -> k h t"), g_[:, :, :], r=[g_],
                          w=[K.dr(("QGS", d_, c))], ds=g_.ds)
                pu = PS[6]
                pg = PS[7]
                for h in range(4):
                    K.pe(lambda e, h=h: e.matmul(pu[:, h * 64:(h + 1) * 64], lhsT=ops_[3][:, h * 128:(h + 1) * 128], rhs=vb[:, h * 64:(h + 1) * 64],
                                                 start=True, stop=True), r=[ops_[3], vb], w=[pu])
                    K.pe(lambda e, h=h: e.matmul(pg[:, h:h + 1], lhsT=logf[:, h * 128:(h + 1) * 128], rhs=ones64[:, :], start=True, stop=True),
                         r=[logf, ones64], w=[pg])
                K.act(lambda e: e.activation(out=dec[:, :], in_=pg[:, 0:4], func=AF.Exp), r=[pg], w=[dec])
                for h in range(4):
                    K.dve(lambda e, h=h: e.scalar_tensor_tensor(out=Sst[:, h, :], in0=Sst[:, h, :], scalar=dec[:, h:h + 1],
                                                                in1=pu[:, h * 64:(h + 1) * 64], op0=ALU.mult, op1=ALU.add),
                          r=[Sst, dec, pu], w=[Sst])
                K.pool(lambda e: e.tensor_copy(out=Sbf[:, :, :], in_=Sst[:, :, :]), r=[Sst], w=[Sbf])
                if latent:
                    K.dve(lambda e: e.tensor_tensor(out=Plog[:, :], in0=Plog[:, :], in1=pg[:, 0:4], op=ALU.add), r=[Plog, pg], w=[Plog])
                    K.act(lambda e: e.activation(out=ePl[:, :], in_=Plog[:, :], func=AF.Exp), r=[Plog], w=[ePl])
            K.dve(lambda e, d_=d_: e.tensor_copy(out=HS[:, d_, 0:256], in_=Sst[:, :, :].rearrange("p h v -> p (h v)")), r=[Sst], w=[HS])
            K.dve(lambda e, d_=d_: e.tensor_copy(out=HS[:, d_, 256:260], in_=Plog[:, :]), r=[Plog], w=[HS])
        K.dma("sp", HSL[:, :], HS[:, :, :].rearrange("p d c -> p (d c)"), r=[HS], w=[K.dr("HSL")], ds=HS.ds)
        for c0 in range(0, NCH, 8):
            c1 = min(NCH, c0 + 8)
            K.dma("sp", OH[c0 * 64:c1 * 64, :].rearrange("(c p) n -> p c n", p=64), OHs[:, c0:c1, :], r=[OHs], w=[K.dr("OH")], ds=OHs.ds)
        allgather(HSL, HSG, ["HSL"], ["HSG"], ccsem2)
        hg = K.alloc("hg", [128, 4, 520], F32, dma=True)
        K.dma("sp", hg[:, :, :], HSG.rearrange("(r p) c -> p r c", p=128), r=[K.dr("HSG")], w=[hg], ds=hg.ds)
        eg = K.alloc("eg", [128, 4], F32)
        for d_ in range(2):
            K.dve(lambda e, d_=d_: e.tensor_copy(out=Sst[:, :, :].rearrange("p h v -> p (h v)"), in_=Sctx[:, d_, :]), r=[Sctx], w=[Sst])
            rorder = range(4) if d_ == 0 else range(3, -1, -1)
            for r_ in rorder:
                mcol = segm[:, d_ * 4 + r_: d_ * 4 + r_ + 1]
                K.act(lambda e, d_=d_, r_=r_: e.activation(out=eg[:, :], in_=hg[:, r_, d_ * 260 + 256: d_ * 260 + 260], func=AF.Exp), r=[hg], w=[eg])
                K.dve(lambda e, mcol=mcol: e.tensor_scalar(out=eg[:, :], in0=eg[:, :], scalar1=-1.0, scalar2=mcol, op0=ALU.add, op1=ALU.mult),
                      r=[eg, segm], w=[eg])
                K.dve(lambda e: e.tensor_scalar(out=eg[:, :], in0=eg[:, :], scalar1=1.0, scalar2=None, op0=ALU.add), r=[eg], w=[eg])
                for h in range(4):
                    K.dve(lambda e, h=h: e.tensor_scalar(out=Sst[:, h, :], in0=Sst[:, h, :], scalar1=eg[:, h:h + 1], scalar2=None, op0=ALU.mult),
                          r=[Sst, eg], w=[Sst])
                    K.dve(lambda e, h=h, d_=d_, r_=r_, mcol=mcol: e.scalar_tensor_tensor(
                        out=Sst[:, h, :], in0=hg[:, r_, d_ * 260 + h * 64: d_ * 260 + (h + 1) * 64], scalar=mcol, in1=Sst[:, h, :],
                        op0=ALU.mult, op1=ALU.add), r=[Sst, hg, segm], w=[Sst])
            K.dve(lambda e, d_=d_: e.tensor_copy(out=S0bf[:, d_, :, :], in_=Sst[:, :, :]), r=[Sst], w=[S0bf])
        K.release(m0)

    S0bf = K.alloc("S0bf", [128, 2, 4, 64], BF16)

    def attn_phase(l, need_ctx):
        m0 = K.mark()
        NKC = NBC * 128
        NKEY = NKC + 4 * T
        QTW = min(512, T)
        KTs = [K.alloc(f"KTs{i}", [96, NKEY], BF16, dma=True) for i in range(2)]
        Vs = [K.alloc(f"Vs{i}", [128, NKB, 65], BF16, dma=True) for i in range(2)]
        Qs = [K.alloc(f"Qs{i}", [96, NTOK], BF16, dma=True) for i in range(2)]
        PT = [K.alloc(f"PT{i}", [128, 512], BF16) for i in range(4)]
        Osb = [K.alloc(f"Osb{i}", [65, 512], F32) for i in range(2)]
        rec = [K.alloc(f"rec{i}", [128, 4], F32) for i in range(2)]
        ob = [K.alloc(f"ob{i}", [128, 4, 64], F32, dma=True) for i in range(2)]
        units = [(s_ * 32, s_ * 32, s_ // 2, 32) for s_ in range(8)] + \
                [(256 + 96 * h, 256 + 96 * h, 4 + h, 96) for h in range(4)] + \
                [(640 + 64 * h, 640 + 64 * (h // 2), 8 + h // 2, 64) for h in range(4)]
        pcount = 0
        ocount = 0
        prevk = prevv = None
        ki = vi = 0
        for u, (qrow, krow, vidx, d) in enumerate(units):
            if (krow) != prevk:
                Kt = KTs[ki % 2]
                ki += 1
                prevk = krow
                K.dma("sp", Kt[0:d, 0:NKC], KTC[krow:krow + d, :], r=[K.dr("KTC")], w=[Kt], ds=Kt.ds)
                for r_ in range(4):
                    K.dma("sp", Kt[0:d, NKC + r_ * T: NKC + (r_ + 1) * T], KTG[kt_row(r_, krow): kt_row(r_, krow) + d, :],
                          r=[K.dr("KTG")], w=[Kt], ds=Kt.ds)
            if vidx != prevv:
                Vt = Vs[vi % 2]
                vi += 1
                prevv = vidx
                K.dma("sp", Vt[:, 0:NBC, :].rearrange("p n c -> p (n c)"), VC[vidx * 128:(vidx + 1) * 128, :],
                      r=[K.dr("VC")], w=[Vt], ds=Vt.ds)
                for r_ in range(4):
                    K.dma("sp", Vt[:, NBC + r_ * NBL: NBC + (r_ + 1) * NBL, :].rearrange("p n c -> p (n c)"),
                          VG[vidx * 512 + r_ * 128: vidx * 512 + (r_ + 1) * 128, :],
                          r=[K.dr("VG")], w=[Vt], ds=Vt.ds)
            Qt = Qs[u % 2]
            K.dma("sp", Qt[0:d, :], QT[qrow:qrow + d, :], r=[K.dr(("QT", b_)) for b_ in range(NB)], w=[Qt], ds=Qt.ds)
            qtiles = [(NKC + t0, min(QTW, T - t0), 0, NKB) for t0 in range(0, T, QTW)]
            if need_ctx:
                qtiles = [(0, NKC, 0, NBC)] + qtiles
            for (q0, qn, kb0, kb1) in qtiles:
                po = PS[4 + ocount % 2]
                LA = 2
                kbs = list(range(kb0, kb1))
                slots = []
                for i in range(len(kbs) + LA):
                    if i < len(kbs):
                        kb = kbs[i]
                        ps = PS[pcount % 4]
                        pt_ = PT[pcount % 4]
                        pcount += 1
                        slots.append((ps, pt_))
                        K.pe(lambda e, ps=ps, Kt=Kt, Qt=Qt, kb=kb, q0=q0, qn=qn, d=d: e.matmul(
                            ps[:, 0:qn], lhsT=Kt[0:d, kb * 128:(kb + 1) * 128], rhs=Qt[0:d, q0:q0 + qn], start=True, stop=True),
                            r=[Kt, Qt], w=[ps])
                    if i >= LA:
                        kb = kbs[i - LA]
                        ps, pt_ = slots[i - LA]
                        K.act(lambda e, ps=ps, pt_=pt_, qn=qn: e.activation(out=pt_[:, 0:qn], in_=ps[:, 0:qn], func=AF.Exp), r=[ps], w=[pt_])
                        K.pe(lambda e, po=po, Vt=Vt, pt_=pt_, kb=kb, qn=qn, kb0=kb0, kb1=kb1: e.matmul(
                            po[0:65, 0:qn], lhsT=Vt[:, kb, :], rhs=pt_[:, 0:qn], start=(kb == kb0), stop=(kb == kb1 - 1)),
                            r=[Vt, pt_], w=[po])
                os_ = Osb[ocount % 2]
                rc = rec[ocount % 2]
                o_ = ob[ocount % 2]
                ocount += 1
                K.dve(lambda e, os_=os_, po=po, qn=qn: e.tensor_copy(out=os_[:, 0:qn], in_=po[0:65, 0:qn]), r=[po], w=[os_])
                ptr = PS[6 + ocount % 2]
                nb_ = qn // 128
                for j in range(nb_):
                    K.pe(lambda e, j=j, os_=os_, ptr=ptr: e.transpose(out=ptr[:, j * 65:(j + 1) * 65], in_=os_[:, j * 128:(j + 1) * 128],
                                                                     identity=ident_f[0:65, 0:65]), r=[os_, ident_f], w=[ptr])
                pv_ = ptr[:, 0:nb_ * 65].rearrange("p (j c) -> p j c", c=65)
                K.dve(lambda e, rc=rc, pv_=pv_, nb_=nb_: e.reciprocal(out=rc[:, 0:nb_], in_=pv_[:, :, 64]), r=[ptr], w=[rc])
                K.dve(lambda e, rc=rc, pv_=pv_, nb_=nb_, o_=o_: e.tensor_tensor(out=o_[:, 0:nb_, :], in0=pv_[:, :, 0:64],
                                                                             in1=rc[:, 0:nb_].unsqueeze(2).broadcast_to([128, nb_, 64]),
                                                                             op=ALU.mult), r=[ptr, rc], w=[o_])
                b0 = q0 // 128
                K.dma("sp", OU[q0:q0 + qn, u * 64:(u + 1) * 64].rearrange("(j p) c -> p j c", p=128), o_[:, 0:nb_, :], r=[o_],
                      w=[K.dr(("OU", u, q0))], ds=o_.ds)
        K.release(m0)

    def mixout_phase(l, xsrc, xdst, blocks):
        m0 = K.mark()
        load_gate(5, 1.0)
        Wo = K.alloc("Wo", [128, 8, D], BF16, dma="sw")
        for k in range(8):
            K.dma("pool", Wo[:, k, :], mix_w_out[l, k * 128:(k + 1) * 128, :], w=[Wo], ds=Wo.ds)
        ou = [K.alloc(f"ou{i}", [128, 16, 64], F32, dma=True) for i in range(2)]
        oh = [K.alloc(f"oh{i}", [128, 256], F32, dma=True) for i in range(2)]
        gg = [K.alloc(f"gg{i}", [128, 256], F32, dma=True) for i in range(2)]
        qg = [K.alloc(f"qg{i}", [128, 2, 4, 128], BF16, dma=True) for i in range(2)]
        xr = [K.alloc(f"oxr{i}", [128, D], F32, dma=True) for i in range(2)]
        y = K.alloc("y", [128, D], F32)
        ybf = K.alloc("ybf", [128, D], BF16)
        yT = K.alloc("yT", [128, 8, 128], BF16)
        dd = K.alloc("dd", [128, 4, 64], F32)
        sq = K.alloc("osq", [128, 4, 64], F32)
        ssb = K.alloc("ossb", [128, 8], F32)
        sgm = K.alloc("osg", [128, 256], F32)
        tmp = K.alloc("otmp", [128, D], F32)
        for n, blk in enumerate(blocks):
            kind = kind_of(blk)
            u_, h_, g_, q_, r_ = ou[n % 2], oh[n % 2], gg[n % 2], qg[n % 2], xr[n % 2]
            rows = slice(blk * 128, (blk + 1) * 128)
            K.dma("sp", u_[:, :, :].rearrange("p u c -> p (u c)"), OU[rows, :],
                  r=[K.dr(("OU", u, q0)) for u in range(16) for q0 in ([0] if kind == 1 else [NBC * 128 + ((blk - NBC) * 128) // min(512, T) * min(512, T)])],
                  w=[u_], ds=u_.ds)
            K.dma("sp", h_[:, :], OH[rows, :], r=[K.dr("OH")], w=[h_], ds=h_.ds)
            K.dma("sp", g_[:, :], ZH[rows, 1792:2048], r=[K.dr(("ZH", blk))], w=[g_], ds=g_.ds)
            K.dma("sp", r_[:, :], xsrc[rows, :], r=[K.dr((id(xsrc), blk))], w=[r_], ds=r_.ds)
            u4 = u_[:, 0:8, :].rearrange("p (h two) c -> p h two c", two=2)
            K.dve(lambda e, u4=u4: e.scalar_tensor_tensor(out=dd[:, :, :], in0=u4[:, :, 1, :], scalar=neglam[:, 0:1], in1=u4[:, :, 0, :],
                                                          op0=ALU.mult, op1=ALU.add), r=[u_, neglam], w=[dd])
            headnorm(dd[:, :, :], 4, 64, gains[:, G_SUB:G_SUB + 256].rearrange("p (h d) -> p h d", d=64),
                     y[:, 0:256].rearrange("p (h d) -> p h d", d=64), sq[:, :, :], ssb[:, 0:4], ssb, K.dve, [dd], [y, sq])
            K.pool(lambda e, u_=u_: e.tensor_copy(out=y[:, 256:512], in_=u_[:, 8:12, :].rearrange("p u c -> p (u c)")), r=[u_], w=[y])
            K.pool(lambda e, u_=u_: e.tensor_copy(out=y[:, 768:1024], in_=u_[:, 12:16, :].rearrange("p u c -> p (u c)")), r=[u_], w=[y])
            if kind == 0:
                for d_ in range(2):
                    K.dma("sp", q_[:, d_, :, :], QGS[d_, :, :, rows].rearrange("h k t -> k h t"),
                          r=[K.dr(("QGS", d_, 2 * blk)), K.dr(("QGS", d_, 2 * blk + 1))], w=[q_], ds=q_.ds)
                pc = PS[0]
                for h in range(4):
                    for d_ in range(2):
                        K.pe(lambda e, h=h, d_=d_, q_=q_: e.matmul(pc[:, h * 64:(h + 1) * 64], lhsT=q_[:, d_, h, :], rhs=S0bf[:, d_, h, :],
                                                                   start=(d_ == 0), stop=(d_ == 1)), r=[q_, S0bf], w=[pc])
                K.dve(lambda e, h_=h_: e.tensor_tensor(out=h_[:, :], in0=h_[:, :], in1=pc[:, 0:256], op=ALU.add), r=[pc, h_], w=[h_])
            headnorm(h_[:, :].rearrange("p (h d) -> p h d", d=64), 4, 64, gains[:, G_HO:G_HO + 256].rearrange("p (h d) -> p h d", d=64),
                     y[:, 512:768].rearrange("p (h d) -> p h d", d=64), sq[:, :, :], ssb[:, 4:8], ssb, K.pool, [h_], [y, sq])
            K.act(lambda e, g_=g_: e.activation(out=sgm[:, :], in_=g_[:, :], func=AF.Sigmoid), r=[g_], w=[sgm])
            K.pool(lambda e, g_=g_: e.tensor_tensor(out=sgm[:, :], in0=sgm[:, :], in1=g_[:, :], op=ALU.mult), r=[sgm, g_], w=[sgm])
            K.pool(lambda e: e.tensor_tensor(out=y[:, 512:768], in0=y[:, 512:768], in1=sgm[:, :], op=ALU.mult), r=[y, sgm], w=[y])
            K.act(lambda e: e.activation(out=ybf[:, :], in_=y[:, :], func=AF.Copy), r=[y], w=[ybf])
            pt = PS[1]
            ptv = pt.ap.bitcast(BF16)
            for c in range(8):
                K.pe(lambda e, c=c: e.transpose(out=ptv[:, c * 128:(c + 1) * 128], in_=ybf[:, c * 128:(c + 1) * 128], identity=ident_bf[:, :]),
                     r=[ybf, ident_bf], w=[pt])
            K.dve(lambda e: e.tensor_copy(out=yT[:, :, :], in_=ptv[:, :].rearrange("p (c t) -> p c t", t=128)), r=[pt], w=[yT])
            for half in range(2):
                py = PS[2 + half]
                for c in range(8):
                    K.pe(lambda e, c=c, py=py, half=half: e.matmul(py[:, :], lhsT=yT[:, c, :], rhs=Wo[:, c, half * 512:(half + 1) * 512],
                                                                  start=(c == 0), stop=(c == 7)), r=[yT, Wo], w=[py])
                K.dve(lambda e, py=py, half=half, kind=kind: e.tensor_tensor(out=tmp[:, half * 512:(half + 1) * 512], in0=py[:, :],
                                                                             in1=gate[kind][:, half * 512:(half + 1) * 512], op=ALU.mult),
                      r=[py, gate[kind]], w=[tmp])
            K.pool(lambda e, r_=r_: e.tensor_tensor(out=r_[:, :], in0=r_[:, :], in1=tmp[:, :], op=ALU.add), r=[r_, tmp], w=[r_])
            K.dma("sp", xdst[rows, :], r_[:, :], r=[r_], w=[K.dr((id(xdst), blk))], ds=r_.ds)
        K.release(m0)

    allblk = list(range(NB))
    latblk = list(range(NBC, NB))
    ph = cfg.phases
    nlayers = L if ph is None else cfg.nlayers_dbg
    xcur = xin
    for l in range(nlayers):
        last = (l == L - 1)
        mod_phase(l)
        layer_consts(l)
        ffn_phase(l, 0, 0, 2, xcur, XA, allblk)
        mixin_phase(l, XA, allblk)
        hgrn_phase(l)
        attn_phase(l, not last)
        blks = latblk if last else allblk
        mixout_phase(l, XA, XB, blks)
        if last:
            ffn_phase(l, 1, 2, 8, XB, None, blks, final_out=True)
        else:
            ffn_phase(l, 1, 2, 8, XB, XA, blks)
        xcur = XA
    if ph is not None:
        cp = K.alloc("cp", [128, D], F32, dma=True)
        srcd = {"XA": XA, "XB": XB, "OU": OU}[cfg.dbg_src]
        for blk in allblk:
            K.dma("sp", cp[:, :], srcd[blk * 128:(blk + 1) * 128, :], r=list(K.dres.values()), w=[cp], ds=cp.ds)
            K.dma("sp", dbg[blk * 128:(blk + 1) * 128, :], cp[:, :], r=[cp], w=[K.dr(("dbg", blk))], ds=cp.ds)
        if not (nlayers == L):
            for lb in range(NBL):
                K.dma("sp", cp[:, :], XA[(NBC + lb) * 128:(NBC + lb + 1) * 128, :], r=list(K.dres.values()), w=[cp], ds=cp.ds)
                K.dma("sp", xout[lb * 128:(lb + 1) * 128, :], cp[:, :], r=[cp], w=[K.dr(("xout", lb))], ds=cp.ds)
    K.wait_all("sp")


def _rope_tables(T, NCTX, j, grid_w=64):
    t = np.arange(j * T, (j + 1) * T)
    r = (t // grid_w).astype(np.float32)
    c = (t % grid_w).astype(np.float32)
    out = np.zeros((NCTX + T, 96), np.float32)
    out[:NCTX, 0:16] = 1.0
    out[:NCTX, 32:64] = 1.0
    for (n_freq, c0, s0) in ((8, 0, 16), (16, 32, 64)):
        inv = (10000.0 ** (-np.arange(n_freq, dtype=np.float32) / n_freq)).astype(np.float32)
        ang = np.concatenate([r[:, None] * inv, c[:, None] * inv], axis=-1).astype(np.float32)
        out[NCTX:, c0:c0 + 2 * n_freq] = np.cos(ang)
        out[NCTX:, s0:s0 + 2 * n_freq] = np.sin(ang)
    return out


def _hmat():
    m = np.zeros((2, 64, 4, 64), np.float32)
    s = np.arange(64)[:, None]
    t = np.arange(64)[None, :]
    for d in range(2):
        MG = (s <= t).astype(np.float32) if d == 0 else (s >= t).astype(np.float32)
        mid = MG[:, 32:33]
        m[d, :, 0, :] = MG - mid
        m[d, :, 1, :] = 1.0 - MG
        m[d, :, 2, :] = MG
        m[d, :, 3, :] = MG
    return m


def make_in_maps(inputs, T, NCTX=256):
    f = lambda a: np.ascontiguousarray(np.asarray(a, dtype=np.float32))
    x, c, ctx, c_ctx = f(inputs["x"]), f(inputs["c"]), f(inputs["ctx"]), f(inputs["c_ctx"])
    L = inputs["ada_w"].shape[0]
    rep = lambda v: np.broadcast_to(np.asarray(v, np.float32).reshape(1, -1), (128, np.asarray(v).size))
    gains = np.zeros((L, 128, 2688), np.float32)
    for l in range(L):
        parts = [np.tile(inputs["da_qk_norm"][l, 0], 8), np.tile(inputs["da_qk_norm"][l, 1], 8),
                 inputs["mla_q_norm"][l], inputs["mla_kv_norm"][l],
                 np.tile(inputs["mla_nope_norm"][l, 0], 4), np.tile(inputs["mla_rope_norm"][l, 0], 4),
                 np.tile(inputs["mla_nope_norm"][l, 1], 4), inputs["mla_rope_norm"][l, 1],
                 np.tile(inputs["gq_qk_norm"][l, 0], 4), np.tile(inputs["gq_qk_norm"][l, 1], 2),
                 np.tile(inputs["da_subln"][l], 4), np.tile(inputs["hg_out_norm"][l], 4)]
        row = np.concatenate([np.asarray(p, np.float32).reshape(-1) for p in parts])
        gains[l, :, :row.size] = row[None, :]
    lamrow = np.stack([rep(inputs["da_lambda"][l]) for l in range(L)])
    lbrow = np.ascontiguousarray(np.broadcast_to(f(inputs["hg_lb_logits"])[None], (128, 2, L, 512)))
    consts = make_consts()
    hm = _hmat()
    shared = dict(ada_w=f(inputs["ada_w"]), ada_b=f(inputs["ada_b"]), norm_w=f(inputs["norm_w"]),
                  ffn_w_in=f(inputs["ffn_w_in"]), ffn_w_out=f(inputs["ffn_w_out"]), mix_w_in=f(inputs["mix_w_in"]),
                  mix_w_out=f(inputs["mix_w_out"]), mla_w_uq=f(inputs["mla_w_uq"]), mla_w_ukv=f(inputs["mla_w_ukv"]),
                  gains=gains, lamrow=np.ascontiguousarray(lamrow), lbrow=lbrow, hmat=hm,
                  ident_bf=consts["ident_bf"], ident_f=consts["ident_f"], sel=consts["sel"])
    maps = []
    for core in range(NCORES):
        b, j = core // 4, core % 4
        segm = np.zeros((128, 8), np.float32)
        for r in range(4):
            segm[:, r] = 1.0 if r < j else 0.0
            segm[:, 4 + r] = 1.0 if r > j else 0.0
        m = dict(shared)
        m.update(xin=np.ascontiguousarray(np.concatenate([ctx[b], x[b, j * T:(j + 1) * T]], 0)),
                 cloc=np.ascontiguousarray(np.stack([c[b], c_ctx])), rope=_rope_tables(T, NCTX, j), segm=segm)
        maps.append(m)
    return maps


_NC_CACHE = {}


def kernel(**inputs):
    B, S, _ = inputs["x"].shape
    T = S // 4
    cfg = Cfg(T=T, NCTX=inputs["ctx"].shape[1], depth=inputs["ada_w"].shape[0])
    nc = build(cfg)
    maps = make_in_maps(inputs, T, cfg.NCTX)
    res = run_bass_kernel_spmd(nc, maps, core_ids=list(range(NCORES)))
    out = np.zeros((B, S, D), np.float32)
    for core in range(NCORES):
        b, j = core // 4, core % 4
        out[b, j * T:(j + 1) * T] = res.results[core]["xout"]
    return out
```

```python
import numpy as np
import ml_dtypes
import concourse.bass as bass
import concourse.mybir as mybir
from concourse.bass_utils import run_bass_kernel_spmd

F32 = mybir.dt.float32
BF16 = mybir.dt.bfloat16
AF = mybir.ActivationFunctionType
ALU = mybir.AluOpType
AX = mybir.AxisListType

D = 1024
DFF = 2816
MIXC = 3744
EPS = 1e-6
NCORES = 8


class Res:
    __slots__ = ("name", "w", "rd")

    def __init__(self, name):
        self.name = name
        self.w = None
        self.rd = {}


class Sem:
    __slots__ = ("h", "count", "name")

    def __init__(self, h, name):
        self.h = h
        self.count = 0
        self.name = name


class Buf:
    def __init__(self, ap, name, ds=None):
        self.ap = ap
        self.r = Res(name)
        self.ds = ds

    def __getitem__(self, k):
        return self.ap[k]


class _Rec:
    def __init__(self):
        self.call = None

    def __getattr__(self, name):
        def f(*a, **k):
            self.call = (name, a, k)
            return self
        return f


class Sched:
    ENG = ("pe", "act", "dve", "pool", "sp")

    def __init__(self, nc, stack):
        self.nc = nc
        self.stack = stack
        self.rec = {e: [] for e in self.ENG}
        self.esem = {e: Sem(stack.enter_context(nc.semaphore("s_" + e)), e) for e in self.ENG}
        self.waited = {e: {} for e in self.ENG}
        self.nsem = 5
        self.dsems = []
        self.off = (nc.sbuf_base + 63) // 64 * 64
        self.sb_cap = nc.sbuf_top
        self.nalloc = 0
        self.dres = {}

    def alloc(self, name, shape, dtype, dma=False):
        esz = 4 if dtype == F32 else 2
        n = 1
        for s in shape[1:]:
            n *= s
        nbytes = (n * esz + 63) // 64 * 64
        assert self.off + nbytes <= self.sb_cap, f"SBUF overflow at {name}: {self.off}+{nbytes}>{self.sb_cap}"
        self.nalloc += 1
        t = self.nc.alloc_sbuf_tensor_at(f"{name}_{self.nalloc}", list(shape), dtype, offset=self.off)
        self.off += nbytes
        ds = None
        if dma:
            if getattr(self, "depth", 0) > 0 and dma != "sw":
                ptr = getattr(self, "pool_ptr", 0)
                ds = self.dsem("pool%d" % ptr)
                self.pool_ptr = ptr + 1
            else:
                ds = self.dsem(name)
        return Buf(t.ap(), name, ds)

    def mark(self):
        self.depth = getattr(self, "depth", 0) + 1
        return (self.off, getattr(self, "pool_ptr", 0))

    def release(self, m):
        self.off, self.pool_ptr = m
        self.depth -= 1
        self.barrier()

    def barrier(self):
        deps = [(s, s.count) for s in self.dsems if s.count] + [(s, s.count) for s in self.esem.values() if s.count]
        for eng in self.ENG:
            wd = self.waited[eng]
            for s, v in deps:
                if wd.get(s, 0) < v:
                    wd[s] = v
                    self.rec[eng].append(("w", s.h, v))

    def dsem(self, name):
        if not hasattr(self, "_dsn"):
            self._dsn = {}
        if name in self._dsn:
            return self._dsn[name]
        s = self._dsn[name] = Sem(self.stack.enter_context(self.nc.semaphore(f"d{len(self.dsems)}_{name}")), name)
        self.dsems.append(s)
        return s

    def dr(self, key):
        r = self.dres.get(key)
        if r is None:
            r = self.dres[key] = Res(str(key))
        return r

    def _wait(self, eng, deps):
        wd = self.waited[eng]
        for s, v in deps:
            if eng == "pe" and s is self.esem["pe"]:
                continue
            if wd.get(s, 0) < v:
                wd[s] = v
                self.rec[eng].append(("w", s.h, v))

    def op(self, eng, fn, r=(), w=(), ds=None):
        deps = []
        for x in r:
            x = x.r if isinstance(x, Buf) else x
            if x.w is not None:
                deps.append(x.w)
        for x in w:
            x = x.r if isinstance(x, Buf) else x
            if x.w is not None and not (ds is not None and x.w[0] is ds):
                deps.append(x.w)
            deps.extend(x.rd.items())
        self._wait(eng, deps)
        if ds is None:
            s = self.esem[eng]
            s.count += 1
            inc = 1
        else:
            s = ds
            s.count += 16
            inc = 16
        ev = (s, s.count)
        p_ = _Rec()
        fn(p_)
        assert p_.call is not None
        import sys as _s
        fr = _s._getframe(1)
        while fr.f_code.co_name in ("op", "pe", "act", "dve", "pool", "dma", "headnorm", "rope"):
            fr = fr.f_back
        self.rec[eng].append(("i", p_.call, s.h, inc, fr.f_lineno))
        for x in r:
            x = x.r if isinstance(x, Buf) else x
            if x.rd.get(s, 0) < s.count:
                x.rd[s] = s.count
        for x in w:
            x = x.r if isinstance(x, Buf) else x
            x.w = ev
            x.rd = {}
        return ev

    def pe(self, fn, r=(), w=()):
        return self.op("pe", fn, r, w)

    def act(self, fn, r=(), w=()):
        return self.op("act", fn, r, w)

    def dve(self, fn, r=(), w=()):
        return self.op("dve", fn, r, w)

    def pool(self, fn, r=(), w=()):
        return self.op("pool", fn, r, w)

    def dma(self, q, out, in_, r=(), w=(), ds=None):
        assert ds is not None
        return self.op(q, lambda e: e.dma_start(out=out, in_=in_), r, w, ds)

    def wait_all(self, eng):
        deps = [(s, s.count) for s in self.dsems if s.count] + [(s, s.count) for s in self.esem.values() if s.count]
        self._wait(eng, [d for d in deps if d[0] is not self.esem[eng]])

    def replay(self):
        nc = self.nc
        hmap = {"pe": "tensor", "act": "scalar", "dve": "vector", "pool": "gpsimd", "sp": "sync"}
        with nc.Block() as block:
            for eng in self.ENG:
                items = self.rec[eng]

                def body(e, items=items):
                    for it in items:
                        if it[0] == "w":
                            e.wait_ge(it[1], it[2])
                        else:
                            nm, a_, k_ = it[1]
                            ins_ = getattr(e, nm)(*a_, **k_)
                            if len(it) > 4:
                                ins_.annotate("L%d" % it[4])
                            ins_.then_inc(it[2], it[3])

                getattr(block, hmap[eng])(body)


def make_consts():
    c = {}
    c["ident_bf"] = np.eye(128, dtype=np.float32).astype(ml_dtypes.bfloat16)
    c["ident_f"] = np.eye(128, dtype=np.float32)
    sel = np.zeros((2, 2, 128), np.float32)
    sel[0, 0, :] = 1.0
    sel[1, 1, :] = 1.0
    c["sel"] = sel
    return c


class Cfg:
    def __init__(self, T=4096, NCTX=256, depth=2, phases=None):
        self.T = T
        self.NCTX = NCTX
        self.NBC = NCTX // 128
        self.NBL = T // 128
        self.NB = self.NBC + self.NBL
        self.NTOK = self.NB * 128
        self.depth = depth
        self.phases = phases
        self.nlayers_dbg = depth
        self.dbg_src = "XA"


def build(cfg):
    from contextlib import ExitStack

    nc = bass.Bass("TRN2", target_bir_lowering=False)
    stack = ExitStack()
    with stack:
        K = Sched(nc, stack)
        _emit(nc, K, cfg)
        K.replay()
    return nc


def _emit(nc, K, cfg):
    NB, NBC, NBL, NTOK, T = cfg.NB, cfg.NBC, cfg.NBL, cfg.NTOK, cfg.T
    L = cfg.depth

    def din(name, shape, dt=F32):
        return nc.dram_tensor(name, list(shape), dt, kind="ExternalInput").ap()

    def dint(name, shape, dt=F32):
        return nc.dram_tensor(name, list(shape), dt, kind="Internal").ap()

    xin = din("xin", [NTOK, D])
    cloc = din("cloc", [2, D])
    ada_w = din("ada_w", [L, D, 9 * D])
    ada_b = din("ada_b", [L, 9 * D])
    norm_w = din("norm_w", [L, 3, D])
    ffn_w_in = din("ffn_w_in", [L, 2, D, 2 * DFF])
    ffn_w_out = din("ffn_w_out", [L, 2, DFF, D])
    ident_bf_d = din("ident_bf", [128, 128], BF16)
    ident_f_d = din("ident_f", [128, 128])
    sel_d = din("sel", [2, 2, 128])
    xout = nc.dram_tensor("xout", [T, D], F32, kind="ExternalOutput").ap()
    XA = dint("XA", [NTOK, D])
    XB = dint("XB", [NTOK, D])
    dbg = None
    if cfg.phases is not None:
        dbg = nc.dram_tensor("dbg", [NTOK, D], F32, kind="ExternalOutput").ap()

    PS = [Buf(nc.alloc_psum_tensor(f"ps{i}", [128, 512], F32).ap(), f"ps{i}") for i in range(8)]

    ident_bf = K.alloc("ident_bf", [128, 128], BF16, dma=True)
    ident_f = K.alloc("ident_f", [128, 128], F32, dma=True)
    sel = K.alloc("sel", [2, 2, 128], F32, dma=True)
    K.dma("sp", ident_bf[:, :], ident_bf_d, w=[ident_bf], ds=ident_bf.ds)
    K.dma("sp", ident_f[:, :], ident_f_d, w=[ident_f], ds=ident_f.ds)
    K.dma("sp", sel[:, :, :], sel_d.rearrange("k r p -> r k p"), w=[sel], ds=sel.ds)
    GR = dint("GR", [2, 3, D])
    modcol = K.alloc("modcol", [128, 48, 2], F32)
    gate = [K.alloc(f"gate{k}", [128, D], F32, dma=True) for k in range(2)]

    def kind_of(blk):
        return 1 if blk < NBC else 0

    def mod_phase(l):
        m0 = K.mark()
        modrow = K.alloc("modrow", [2, 9 * D], F32)
        craw = K.alloc("craw", [2, D], F32, dma=True)
        csil = K.alloc("csil", [2, D], F32)
        csilT = K.alloc("csilT", [128, 8, 2], F32)
        adab = K.alloc("adab", [2, 9 * D], F32, dma=True)
        nw = K.alloc("nw", [2, 3, D], F32, dma=True)
        abrow = K.alloc("abrow", [2, 3, D], F32)
        wa = [K.alloc(f"wa{i}", [128, 8, 512], F32, dma=True) for i in range(2)]
        K.dma("sp", craw[:, :], cloc, w=[craw], ds=craw.ds)
        for rr in range(2):
            K.dma("sp", adab[rr:rr + 1, :], ada_b[l:l + 1, :], w=[adab], ds=adab.ds)
            K.dma("sp", nw[rr:rr + 1, :, :], norm_w[l:l + 1, :, :], w=[nw], ds=nw.ds)
        K.act(lambda e: e.activation(out=csil[:, :], in_=craw[:, :], func=AF.Silu), r=[craw], w=[csil])
        pt = PS[0]
        for k in range(8):
            K.pe(lambda e, k=k: e.transpose(out=pt[:, k * 2:(k + 1) * 2], in_=csil[0:2, k * 128:(k + 1) * 128],
                                            identity=ident_f[0:2, 0:2]), r=[csil, ident_f], w=[pt])
        K.dve(lambda e: e.tensor_copy(out=csilT[:, :, :], in_=pt[:, 0:16].rearrange("p (k r) -> p k r", r=2)),
              r=[pt], w=[csilT])
        for cg in range(18):
            w_ = wa[cg % 2]
            K.dma("sp", w_[:, :, :], ada_w[l, :, cg * 512:(cg + 1) * 512].rearrange("(k p) n -> p k n", p=128),
                  w=[w_], ds=w_.ds)
            pm = PS[1 + cg % 2]
            for k in range(8):
                K.pe(lambda e, k=k, pm=pm, w_=w_: e.matmul(pm[0:2, :], lhsT=csilT[:, k, :], rhs=w_[:, k, :],
                                                          start=(k == 0), stop=(k == 7)),
                     r=[csilT, w_], w=[pm])
            K.dve(lambda e, pm=pm, cg=cg: e.tensor_tensor(out=modrow[:, cg * 512:(cg + 1) * 512], in0=pm[0:2, :],
                                                          in1=adab[:, cg * 512:(cg + 1) * 512], op=ALU.add),
                  r=[pm, adab], w=[modrow])
        for i in range(3):
            K.dve(lambda e, i=i: e.scalar_tensor_tensor(out=abrow[:, i, :], in0=modrow[:, (3 * i + 1) * D:(3 * i + 2) * D],
                                                        scalar=1.0, in1=nw[:, i, :], op0=ALU.add, op1=ALU.mult),
                  r=[modrow, nw], w=[abrow])
        pt = PS[3]
        for i in range(3):
            for which in range(2):
                for c in range(8):
                    idx = (i * 2 + which) * 8 + c
                    if which == 0:
                        src = abrow[0:2, i, c * 128:(c + 1) * 128]
                        rr = [abrow, ident_f]
                    else:
                        src = modrow[0:2, 3 * i * D + c * 128: 3 * i * D + (c + 1) * 128]
                        rr = [modrow, ident_f]
                    K.pe(lambda e, idx=idx, src=src: e.transpose(out=pt[:, idx * 2:(idx + 1) * 2], in_=src,
                                                                 identity=ident_f[0:2, 0:2]), r=rr, w=[pt])
        K.dve(lambda e: e.tensor_copy(out=modcol[:, :, :], in_=pt[:, 0:96].rearrange("p (a r) -> p a r", r=2)),
              r=[pt], w=[modcol])
        gds = K.dsem("grst")
        for jj in range(3):
            K.dma("sp", GR[:, jj, :], modrow[:, (3 * jj + 2) * D:(3 * jj + 3) * D], r=[modrow], w=[K.dr(("GR", jj))], ds=gds)
        K.release(m0)

    def load_gate(j, scale):
        for kind in range(2):
            g = gate[kind]
            K.dma("sp", g[:, :], GR[kind, j // 3:j // 3 + 1, :].broadcast_to([128, D]), r=[K.dr(("GR", j // 3))], w=[g], ds=g.ds)
            if scale != 1.0:
                K.dve(lambda e, g=g: e.tensor_scalar(out=g[:, :], in0=g[:, :], scalar1=scale, scalar2=None, op0=ALU.mult), r=[g], w=[g])

    def Acol(i, kind, c):
        return modcol[:, (i * 2 + 0) * 8 + c, kind:kind + 1]

    def Bcol(i, kind, c):
        return modcol[:, (i * 2 + 1) * 8 + c, kind:kind + 1]

    rstd_all = K.alloc("rstd_all", [128, NB], F32)
    ss_all = K.alloc("ss_all", [128, NB], F32)

    def stats_prepass(xsrc, blocks):
        m0 = K.mark()
        xs = [K.alloc(f"xs{i}", [128, D], F32, dma=True) for i in range(2)]
        jk = K.alloc("sjunk", [128, D], BF16)
        for n, blk in enumerate(blocks):
            b_ = xs[n % 2]
            K.dma("sp", b_[:, :], xsrc[blk * 128:(blk + 1) * 128, :], r=[K.dr((id(xsrc), blk))], w=[b_], ds=b_.ds)
            K.act(lambda e, b_=b_, blk=blk: e.activation(out=jk[:, :], in_=b_[:, :], func=AF.Square,
                                                        accum_out=ss_all[:, blk:blk + 1]), r=[b_], w=[jk, ss_all])
        K.act(lambda e: e.activation(out=rstd_all[:, :], in_=ss_all[:, :], func=AF.Ln, scale=1.0 / D, bias=EPS),
              r=[ss_all], w=[rstd_all])
        K.act(lambda e: e.activation(out=rstd_all[:, :], in_=rstd_all[:, :], func=AF.Exp, scale=-0.5),
              r=[rstd_all], w=[rstd_all])
        K.release(m0)

    def modnorm_T(xb, i, kind, hT_dst, tmp, blk):
        xn, pst = tmp
        K.dve(lambda e: e.tensor_scalar(out=xn[:, :], in0=xb[:, :], scalar1=rstd_all[:, blk:blk + 1], scalar2=None,
                                        op0=ALU.mult), r=[xb, rstd_all], w=[xn])
        pv = pst.ap.bitcast(BF16)
        for c in range(8):
            K.pe(lambda e, c=c: e.transpose(out=pv[:, c * 128:(c + 1) * 128], in_=xn[:, c * 128:(c + 1) * 128],
                                            identity=ident_bf[:, :]), r=[xn, ident_bf], w=[pst])
        for c in range(8):
            K.act(lambda e, c=c: e.activation(out=hT_dst(c), in_=pv[:, c * 128:(c + 1) * 128], func=AF.Identity,
                                              scale=Acol(i, kind, c), bias=Bcol(i, kind, c)),
                  r=[pst, modcol], w=[hT_dst.buf])

    def ffn_phase(l, s, i_norm, j_gate, xsrc, xdst, blocks, final_out=False):
        stats_prepass(xsrc, blocks)
        m0 = K.mark()
        load_gate(j_gate, 0.5)
        Win = K.alloc("Win", [128, 8, 2 * DFF], BF16, dma="sw")
        Wout = K.alloc("Wout", [128, 22, D], BF16, dma="sw")
        for k in range(8):
            K.dma("pool", Win[:, k, :], ffn_w_in[l, s, k * 128:(k + 1) * 128, :], w=[Win], ds=Win.ds)
        for k in range(22):
            K.dma("pool", Wout[:, k, :], ffn_w_out[l, s, k * 128:(k + 1) * 128, :], w=[Wout], ds=Wout.ds)
        xb = [K.alloc(f"xb{i}", [128, D], F32, dma=True) for i in range(2)]
        xr = [K.alloc(f"xr{i}", [128, D], F32, dma=True) for i in range(2)]
        xn = [K.alloc(f"xn{i}", [128, D], BF16) for i in range(2)]
        hT = [K.alloc(f"hT{i}", [128, 8, 256], BF16) for i in range(2)]
        aT = K.alloc("aT", [128, 22, 256], BF16)
        sg = [K.alloc(f"sg{i}", [128, 256], F32) for i in range(2)]
        tmp = [K.alloc("ytmp", [128, D], F32)] * 2
        tiles = [blocks[a:a + 2] for a in range(0, len(blocks), 2)]
        cnt = 0
        for ti, tile in enumerate(tiles):
            h = hT[ti % 2]
            for bi, blk in enumerate(tile):
                b_ = xb[cnt % 2]
                K.dma("sp", b_[:, :], xsrc[blk * 128:(blk + 1) * 128, :], r=[K.dr((id(xsrc), blk))], w=[b_], ds=b_.ds)

                def dst(c, h=h, bi=bi):
                    return h[:, c, bi * 128:(bi + 1) * 128]
                dst.buf = h
                modnorm_T(b_, i_norm, kind_of(blk), dst, (xn[cnt % 2], PS[0]), blk)
                cnt += 1
            nt = 128 * len(tile)
            for fc in range(22):
                pg = PS[1 + 2 * (fc % 2)]
                pu = PS[2 + 2 * (fc % 2)]
                for k in range(8):
                    K.pe(lambda e, k=k, pg=pg, fc=fc, h=h: e.matmul(pg[:, 0:nt], lhsT=Win[:, k, fc * 128:(fc + 1) * 128],
                                                                   rhs=h[:, k, 0:nt], start=(k == 0), stop=(k == 7)),
                         r=[Win, h], w=[pg])
                for k in range(8):
                    K.pe(lambda e, k=k, pu=pu, fc=fc, h=h: e.matmul(pu[:, 0:nt], lhsT=Win[:, k, DFF + fc * 128:DFF + (fc + 1) * 128],
                                                                   rhs=h[:, k, 0:nt], start=(k == 0), stop=(k == 7)),
                         r=[Win, h], w=[pu])
                s_ = sg[fc % 2]
                K.act(lambda e, pg=pg, s_=s_: e.activation(out=s_[:, 0:nt], in_=pg[:, 0:nt], func=AF.Silu),
                      r=[pg], w=[s_])
                K.dve(lambda e, pu=pu, s_=s_, fc=fc: e.tensor_tensor(out=aT[:, fc, 0:nt], in0=pu[:, 0:nt], in1=s_[:, 0:nt],
                                                                     op=ALU.mult), r=[pu, s_], w=[aT])
            for bi, blk in enumerate(tile):
                kind = kind_of(blk)
                t_ = tmp[bi % 2]
                for half in range(2):
                    py = PS[5 + half]
                    for fc in range(22):
                        K.pe(lambda e, fc=fc, py=py, bi=bi, half=half: e.matmul(
                            py[:, :], lhsT=aT[:, fc, bi * 128:(bi + 1) * 128], rhs=Wout[:, fc, half * 512:(half + 1) * 512],
                            start=(fc == 0), stop=(fc == 21)), r=[aT, Wout], w=[py])
                    K.dve(lambda e, py=py, half=half, t_=t_, kind=kind: e.tensor_tensor(
                        out=t_[:, half * 512:(half + 1) * 512], in0=py[:, :], in1=gate[kind][:, half * 512:(half + 1) * 512],
                        op=ALU.mult), r=[py, gate[kind]], w=[t_])
                r_ = xr[bi % 2]
                o_ = r_
                K.dma("sp", r_[:, :], xsrc[blk * 128:(blk + 1) * 128, :], r=[K.dr((id(xsrc), blk))], w=[r_], ds=r_.ds)
                K.pool(lambda e, r_=r_, o_=o_, t_=t_: e.tensor_tensor(out=o_[:, :], in0=r_[:, :], in1=t_[:, :], op=ALU.add),
                       r=[r_, t_], w=[o_])
                if final_out:
                    lb = blk - NBC
                    K.dma("sp", xout[lb * 128:(lb + 1) * 128, :], o_[:, :], r=[o_], w=[K.dr(("xout", lb))], ds=o_.ds)
                else:
                    K.dma("sp", xdst[blk * 128:(blk + 1) * 128, :], o_[:, :], r=[o_], w=[K.dr((id(xdst), blk))], ds=o_.ds)
        K.release(m0)

    NKB = NBC + 4 * NBL
    NCH = NB * 2
    GW = 2688
    mix_w_in = din("mix_w_in", [L, D, MIXC])
    mix_w_out = din("mix_w_out", [L, D, D])
    w_uq_d = din("mla_w_uq", [L, 256, 384])
    w_ukv_d = din("mla_w_ukv", [L, 128, 512])
    gains_d = din("gains", [L, 128, GW])
    lamrow_d = din("lamrow", [L, 128, 128])
    lbrow_d = din("lbrow", [128, 2, L, 512])
    rope_d = din("rope", [NTOK, 96])
    hmat_d = din("hmat", [2, 64, 4, 64])
    segm_d = din("segm", [128, 8])
    QT = dint("QT", [896, NTOK], BF16)
    KTL = dint("KTL", [768, T], BF16)
    KTC = dint("KTC", [768, NBC * 128], BF16)
    VL = dint("VL", [10 * 128, NBL * 65], BF16)
    VC = dint("VC", [10 * 128, NBC * 65], BF16)
    KTG = dint("KTG", [4 * 768, T], BF16)
    VG = dint("VG", [4 * 10 * 128, NBL * 65], BF16)
    ZH = dint("ZH", [NTOK, 2048])
    OH = dint("OH", [NTOK, 256])
    QGS = dint("QGS", [2, 4, 128, NTOK], BF16)
    HSL = dint("HSL", [128, 520])
    HSG = dint("HSG", [4 * 128, 520])
    OU = dint("OU", [NTOK, 1024])
    RG = [[0, 1, 2, 3], [4, 5, 6, 7]]
    ccsem = K.dsem("cc")
    ccsem2 = K.dsem("cc2")
    KPB = [0, 64, 128, 192, 256, 352, 448, 544, 640, 704, 768]

    def kt_row(r_, krow):
        for a, b in zip(KPB[:-1], KPB[1:]):
            if a <= krow < b:
                return 4 * a + r_ * (b - a) + (krow - a)
        raise AssertionError

    def allgather(src, dst, rkeys, wkeys, s=None):
        s = ccsem if s is None else s
        deps = []
        for k in rkeys:
            x = K.dr(k)
            if x.w is not None:
                deps.append(x.w)
        for k in wkeys:
            x = K.dr(k)
            if x.w is not None:
                deps.append(x.w)
            deps.extend(x.rd.items())
        K._wait("pool", deps)
        s.count += 1
        K.rec["pool"].append(("i", ("collective_compute", ("AllGather", ALU.bypass),
                                    dict(replica_groups=RG, ins=[src.opt()], outs=[dst.opt()])), s.h, 1))
        ev = (s, s.count)
        for k in wkeys:
            x = K.dr(k)
            x.w = ev
            x.rd = {}
        for k in rkeys:
            K.dr(k).rd[s] = s.count

    gains = K.alloc("gains", [128, GW], F32, dma=True)
    hmat = K.alloc("hmat", [64, 2, 4, 64], F32, dma=True)
    hmask = K.alloc("hmask", [64, 2, 64], BF16)
    segm = K.alloc("segm", [128, 8], F32, dma=True)
    neglam = K.alloc("neglam", [128, 1], F32)
    ones64 = K.alloc("ones64", [64, 1], F32)
    K.dma("sp", hmat[:, :, :, :], hmat_d.rearrange("d s m t -> s d m t"), w=[hmat], ds=hmat.ds)
    K.dma("sp", segm[:, :], segm_d, w=[segm], ds=segm.ds)
    K.dve(lambda e: e.memset(ones64[:, :], 1.0), w=[ones64])
    for d_ in range(2):
        K.dve(lambda e, d_=d_: e.tensor_copy(out=hmask[:, d_, :], in_=hmat[:, d_, 3, :]), r=[hmat], w=[hmask])

    G_AQK, G_BQ, G_BKV, G_BNQ, G_BRQ, G_BNK, G_BRK, G_D, G_SUB, G_HO = 0, 512, 768, 896, 1152, 1280, 1536, 1568, 1952, 2208

    def layer_consts(l):
        lam_init = 0.8 - 0.6 * float(np.exp(-0.3 * l))
        m0 = K.mark()
        K.dma("sp", gains[:, :], gains_d[l], w=[gains], ds=gains.ds)
        lr = K.alloc("lr", [128, 128], F32, dma=True)
        t1 = K.alloc("t1", [128, 2, 32], F32)
        s12 = K.alloc("s12", [128, 2], F32)
        K.dma("sp", lr[:, :], lamrow_d[l], w=[lr], ds=lr.ds)
        lv = lr.ap.rearrange("p (a b) -> p a b", b=32)
        K.dve(lambda e: e.tensor_tensor(out=t1[:, 0, :], in0=lv[:, 0, :], in1=lv[:, 1, :], op=ALU.mult), r=[lr], w=[t1])
        K.dve(lambda e: e.tensor_tensor(out=t1[:, 1, :], in0=lv[:, 2, :], in1=lv[:, 3, :], op=ALU.mult), r=[lr], w=[t1])
        K.dve(lambda e: e.tensor_reduce(out=s12[:, :], in_=t1[:, :, :], axis=AX.X, op=ALU.add), r=[t1], w=[s12])
        K.act(lambda e: e.activation(out=s12[:, :], in_=s12[:, :], func=AF.Exp), r=[s12], w=[s12])
        K.dve(lambda e: e.scalar_tensor_tensor(out=neglam[:, :], in0=s12[:, 1:2], scalar=-lam_init, in1=s12[:, 0:1],
                                               op0=ALU.add, op1=ALU.subtract), r=[s12], w=[neglam])
        K.dve(lambda e: e.tensor_scalar(out=gains[:, G_AQK:G_AQK + 256], in0=gains[:, G_AQK:G_AQK + 256], scalar1=32 ** -0.5,
                                        scalar2=None, op0=ALU.mult), r=[gains], w=[gains])
        K.dve(lambda e: e.tensor_scalar(out=gains[:, G_BNQ:G_BNQ + 384], in0=gains[:, G_BNQ:G_BNQ + 384], scalar1=96 ** -0.5,
                                        scalar2=None, op0=ALU.mult), r=[gains], w=[gains])
        K.dve(lambda e: e.tensor_scalar(out=gains[:, G_D:G_D + 256], in0=gains[:, G_D:G_D + 256], scalar1=64 ** -0.5,
                                        scalar2=None, op0=ALU.mult), r=[gains], w=[gains])
        K.dve(lambda e: e.tensor_scalar(out=gains[:, G_SUB:G_SUB + 256], in0=gains[:, G_SUB:G_SUB + 256], scalar1=1.0 - lam_init,
                                        scalar2=None, op0=ALU.mult), r=[gains], w=[gains])
        K.release(m0)

    def rsq(e_, ss, d):
        e_(lambda e: e.activation(out=ss, in_=ss, func=AF.Ln, scale=1.0 / d, bias=EPS))
        e_(lambda e: e.activation(out=ss, in_=ss, func=AF.Exp, scale=-0.5))

    def headnorm(src, H, d, gain, dst, sq, ss, ssbuf, eng, rr, ww):
        eng(lambda e: e.tensor_tensor(out=sq, in0=src, in1=src, op=ALU.mult), r=rr, w=[ww[1]])
        K.dve(lambda e: e.tensor_reduce(out=ss, in_=sq, axis=AX.X, op=ALU.add), r=[ww[1]], w=[ssbuf])
        K.act(lambda e: e.activation(out=ss, in_=ss, func=AF.Ln, scale=1.0 / d, bias=EPS), r=[ssbuf], w=[ssbuf])
        K.act(lambda e: e.activation(out=ss, in_=ss, func=AF.Exp, scale=-0.5), r=[ssbuf], w=[ssbuf])
        np_ = ss.shape[0]
        eng(lambda e: e.tensor_tensor(out=sq, in0=src, in1=ss.unsqueeze(2).broadcast_to([np_, H, d]), op=ALU.mult),
            r=rr + [ssbuf], w=[ww[1]])
        eng(lambda e: e.tensor_tensor(out=dst, in0=sq, in1=gain, op=ALU.mult), r=[ww[1], gains], w=[ww[0]])

    ROPE = [None]

    def rope(src, H, hd, cos, sin, dst, ta, tb, eng, rr, ww, tw):
        ropeT = ROPE[0]
        cb = cos.unsqueeze(1).broadcast_to([128, H, hd])
        sb_ = sin.unsqueeze(1).broadcast_to([128, H, hd])
        x1, x2 = src[:, :, 0, :], src[:, :, 1, :]
        eng(lambda e: e.tensor_tensor(out=ta, in0=x1, in1=cb, op=ALU.mult), r=rr + [ropeT], w=[tw[0]])
        eng(lambda e: e.tensor_tensor(out=tb, in0=x2, in1=sb_, op=ALU.mult), r=rr + [ropeT], w=[tw[1]])
        eng(lambda e: e.tensor_tensor(out=dst[:, :, 0, :], in0=ta, in1=tb, op=ALU.subtract), r=[tw[0], tw[1]], w=ww)
        eng(lambda e: e.tensor_tensor(out=ta, in0=x1, in1=sb_, op=ALU.mult), r=rr + [ropeT], w=[tw[0]])
        eng(lambda e: e.tensor_tensor(out=tb, in0=x2, in1=cb, op=ALU.mult), r=rr + [ropeT], w=[tw[1]])
        eng(lambda e: e.tensor_tensor(out=dst[:, :, 1, :], in0=ta, in1=tb, op=ALU.add), r=[tw[0], tw[1]], w=ww)

    def mixin_phase(l, xsrc, blocks):
        stats_prepass(xsrc, blocks)
        m0 = K.mark()
        ropeT = K.alloc("ropeT", [128, NB, 96], F32, dma=True)
        for n0 in range(0, NB, 8):
            n1 = min(NB, n0 + 8)
            K.dma("sp", ropeT[:, n0:n1, :], rope_d[n0 * 128:n1 * 128, :].rearrange("(n p) c -> p n c", p=128), w=[ropeT], ds=ropeT.ds)
        ROPE[0] = ropeT
        Wm = K.alloc("Wm", [128, 8, MIXC], BF16, dma="sw")
        wuq = K.alloc("wuq", [128, 2, 384], BF16, dma="sw")
        wukv = K.alloc("wukv", [128, 512], BF16, dma="sw")
        for k in range(8):
            K.dma("pool", Wm[:, k, :], mix_w_in[l, k * 128:(k + 1) * 128, :], w=[Wm], ds=Wm.ds)
        for k in range(2):
            K.dma("pool", wuq[:, k, :], w_uq_d[l, k * 128:(k + 1) * 128, :], w=[wuq], ds=wuq.ds)
        K.dma("pool", wukv[:, :], w_ukv_d[l], w=[wukv], ds=wukv.ds)
        xb = [K.alloc(f"mxb{i}", [128, D], F32, dma=True) for i in range(2)]
        xn = K.alloc("mxn", [128, D], BF16)
        hT = [K.alloc(f"mhT{i}", [128, 8, 128], BF16) for i in range(2)]
        zs = [K.alloc(f"zs{i}", [128, MIXC], F32, dma=True) for i in range(2)]
        sq = K.alloc("sq", [128, 1024], F32)
        wk = K.alloc("wk", [128, 1024], F32)
        ta = K.alloc("ta", [128, 512], F32)
        tb = K.alloc("tb", [128, 512], F32)
        ssb = K.alloc("ssb", [128, 32], F32)
        qa = K.alloc("qa", [128, 512], BF16)
        qd = K.alloc("qd", [128, 384], BF16)
        lat = K.alloc("lat", [128, 384], BF16)
        latT = K.alloc("latT", [128, 3, 128], BF16)
        bq = K.alloc("bq", [128, 384], F32)
        bkv = K.alloc("bkv", [128, 512], F32)
        bqo = K.alloc("bqo", [128, 4, 96], BF16)
        bko = K.alloc("bko", [128, 4, 96], BF16)
        krr = K.alloc("krr", [128, 32], F32)
        qst = [K.alloc(f"qst{i}", [128, 7, 128], BF16, dma=True) for i in range(2)]
        kst = [K.alloc(f"kst{i}", [128, 6, 128], BF16, dma=True) for i in range(2)]
        vst = [K.alloc(f"vst{i}", [128, 10, 65], BF16, dma=True) for i in range(2)]
        for v_ in vst:
            K.dve(lambda e, v_=v_: e.memset(v_[:, :, :], 1.0), w=[v_])
        groups = [(0, 512), (512, 768), (768, 1184), (1184, 1696), (1696, 2208), (2208, 2720), (2720, 3232), (3232, 3744)]
        for n, blk in enumerate(blocks):
            kind = kind_of(blk)
            b_ = xb[n % 2]
            h = hT[n % 2]
            z = zs[n % 2]
            K.dma("sp", b_[:, :], xsrc[blk * 128:(blk + 1) * 128, :], r=[K.dr((id(xsrc), blk))], w=[b_], ds=b_.ds)

            def dst(c, h=h):
                return h[:, c, :]
            dst.buf = h
            modnorm_T(b_, 1, kind, dst, (xn, PS[0]), blk)
            for gi, (c0, c1) in enumerate(groups):
                pz = PS[1 + gi % 3]
                for k in range(8):
                    K.pe(lambda e, k=k, pz=pz, c0=c0, c1=c1, h=h: e.matmul(pz[:, 0:c1 - c0], lhsT=h[:, k, :], rhs=Wm[:, k, c0:c1],
                                                                         start=(k == 0), stop=(k == 7)), r=[h, Wm], w=[pz])
                if gi % 2 == 0:
                    K.act(lambda e, pz=pz, c0=c0, c1=c1, z=z: e.activation(out=z[:, c0:c1], in_=pz[:, 0:c1 - c0], func=AF.Copy),
                          r=[pz], w=[z])
                else:
                    K.dve(lambda e, pz=pz, c0=c0, c1=c1, z=z: e.tensor_copy(out=z[:, c0:c1], in_=pz[:, 0:c1 - c0]), r=[pz], w=[z])
            K.dma("sp", ZH[blk * 128:(blk + 1) * 128, :], z[:, 1184:3232], r=[z], w=[K.dr(("ZH", blk))], ds=z.ds)
            cosA, sinA = ropeT[:, blk, 0:16], ropeT[:, blk, 16:32]
            cosD, sinD = ropeT[:, blk, 32:64], ropeT[:, blk, 64:96]
            q_ = qst[n % 2]
            k_ = kst[n % 2]
            v_ = vst[n % 2]
            headnorm(z[:, 0:512].rearrange("p (h d) -> p h d", d=32), 16, 32,
                     gains[:, G_AQK:G_AQK + 512].rearrange("p (h d) -> p h d", d=32),
                     wk[:, 0:512].rearrange("p (h d) -> p h d", d=32), sq[:, 0:512].rearrange("p (h d) -> p h d", d=32),
                     ssb[:, 0:16], ssb, K.dve, [z], [wk, sq])
            rope(wk[:, 0:512].rearrange("p (h two d) -> p h two d", two=2, d=16), 16, 16, cosA, sinA,
                 qa[:, :].rearrange("p (h two d) -> p h two d", two=2, d=16),
                 ta[:, 0:256].rearrange("p (h d) -> p h d", d=16), tb[:, 0:256].rearrange("p (h d) -> p h d", d=16),
                 K.dve, [wk], [qa], [ta, tb])
            headnorm(z[:, 3232:3616].rearrange("p (h d) -> p h d", d=64), 6, 64,
                     gains[:, G_D:G_D + 384].rearrange("p (h d) -> p h d", d=64),
                     wk[:, 512:896].rearrange("p (h d) -> p h d", d=64), sq[:, 512:896].rearrange("p (h d) -> p h d", d=64),
                     ssb[:, 16:22], ssb, K.pool, [z], [wk, sq])
            rope(wk[:, 512:896].rearrange("p (h two d) -> p h two d", two=2, d=32), 6, 32, cosD, sinD,
                 qd[:, :].rearrange("p (h two d) -> p h two d", two=2, d=32),
                 ta[:, 256:448].rearrange("p (h d) -> p h d", d=32), tb[:, 256:448].rearrange("p (h d) -> p h d", d=32),
                 K.pool, [wk], [qd], [ta, tb])
            headnorm(z[:, 768:1024].rearrange("p (h d) -> p h d", d=256), 1, 256,
                     gains[:, G_BQ:G_BQ + 256].rearrange("p (h d) -> p h d", d=256),
                     lat[:, 0:256].rearrange("p (h d) -> p h d", d=256), sq[:, 0:256].rearrange("p (h d) -> p h d", d=256),
                     ssb[:, 22:23], ssb, K.dve, [z], [lat, sq])
            headnorm(z[:, 1024:1152].rearrange("p (h d) -> p h d", d=128), 1, 128,
                     gains[:, G_BKV:G_BKV + 128].rearrange("p (h d) -> p h d", d=128),
                     lat[:, 256:384].rearrange("p (h d) -> p h d", d=128), sq[:, 256:384].rearrange("p (h d) -> p h d", d=128),
                     ssb[:, 23:24], ssb, K.dve, [z], [lat, sq])
            pt = PS[4]
            ptv = pt.ap.bitcast(BF16)
            for c in range(3):
                K.pe(lambda e, c=c: e.transpose(out=ptv[:, c * 128:(c + 1) * 128], in_=lat[:, c * 128:(c + 1) * 128],
                                                identity=ident_bf[:, :]), r=[lat, ident_bf], w=[pt])
            K.act(lambda e: e.activation(out=latT[:, :, :], in_=ptv[:, 0:384].rearrange("p (c t) -> p c t", t=128), func=AF.Copy),
                  r=[pt], w=[latT])
            pq = PS[5]
            for c in range(2):
                K.pe(lambda e, c=c: e.matmul(pq[:, 0:384], lhsT=latT[:, c, :], rhs=wuq[:, c, :], start=(c == 0), stop=(c == 1)),
                     r=[latT, wuq], w=[pq])
            K.act(lambda e: e.activation(out=bq[:, :], in_=pq[:, 0:384], func=AF.Copy), r=[pq], w=[bq])
            pk = PS[6]
            K.pe(lambda e: e.matmul(pk[:, :], lhsT=latT[:, 2, :], rhs=wukv[:, :], start=True, stop=True), r=[latT, wukv], w=[pk])
            K.act(lambda e: e.activation(out=bkv[:, :], in_=pk[:, :], func=AF.Copy), r=[pk], w=[bkv])
            bq3 = bq[:, :].rearrange("p (h d) -> p h d", d=96)
            bkv3 = bkv[:, :].rearrange("p (h d) -> p h d", d=128)
            headnorm(bq3[:, :, 0:64], 4, 64, gains[:, G_BNQ:G_BNQ + 256].rearrange("p (h d) -> p h d", d=64),
                     bqo[:, :, 0:64], sq[:, 0:256].rearrange("p (h d) -> p h d", d=64), ssb[:, 24:28], ssb, K.dve, [bq], [bqo, sq])
            headnorm(bq3[:, :, 64:96], 4, 32, gains[:, G_BRQ:G_BRQ + 128].rearrange("p (h d) -> p h d", d=32),
                     wk[:, 896:1024].rearrange("p (h d) -> p h d", d=32), sq[:, 256:384].rearrange("p (h d) -> p h d", d=32),
                     ssb[:, 28:32], ssb, K.dve, [bq], [wk, sq])
            rope(wk[:, 896:1024].rearrange("p (h two d) -> p h two d", two=2, d=16), 4, 16, cosA, sinA,
                 bqo[:, :, 64:96].rearrange("p h (two d) -> p h two d", two=2),
                 ta[:, 448:512].rearrange("p (h d) -> p h d", d=16), tb[:, 448:512].rearrange("p (h d) -> p h d", d=16),
                 K.dve, [wk], [bqo], [ta, tb])
            headnorm(bkv3[:, :, 0:64], 4, 64, gains[:, G_BNK:G_BNK + 256].rearrange("p (h d) -> p h d", d=64),
                     bko[:, :, 0:64], sq[:, 384:640].rearrange("p (h d) -> p h d", d=64), ssb[:, 24:28], ssb, K.pool, [bkv], [bko, sq])
            headnorm(z[:, 1152:1184].rearrange("p (h d) -> p h d", d=32), 1, 32,
                     gains[:, G_BRK:G_BRK + 32].rearrange("p (h d) -> p h d", d=32),
                     krr[:, :].rearrange("p (h d) -> p h d", d=32), sq[:, 640:672].rearrange("p (h d) -> p h d", d=32),
                     ssb[:, 22:23], ssb, K.pool, [z], [krr, sq])
            rope(krr[:, :].rearrange("p (h two d) -> p h two d", two=2, d=16), 1, 16, cosA, sinA,
                 bko[:, 0:1, 64:96].rearrange("p h (two d) -> p h two d", two=2),
                 ta[:, 448:464].rearrange("p (h d) -> p h d", d=16), tb[:, 448:464].rearrange("p (h d) -> p h d", d=16),
                 K.pool, [krr], [bko], [ta, tb])
            for hh in range(1, 4):
                K.pool(lambda e, hh=hh: e.tensor_copy(out=bko[:, hh, 64:96], in_=bko[:, 0, 64:96]), r=[bko], w=[bko])
            K.act(lambda e, v_=v_, z=z: e.activation(out=v_[:, 0:4, 0:64], in_=z[:, 512:768].rearrange("p (h d) -> p h d", d=64),
                                                     func=AF.Copy), r=[z], w=[v_])
            K.act(lambda e, v_=v_: e.activation(out=v_[:, 4:8, 0:64], in_=bkv3[:, :, 64:128], func=AF.Copy), r=[bkv], w=[v_])
            K.act(lambda e, v_=v_, z=z: e.activation(out=v_[:, 8:10, 0:64], in_=z[:, 3616:3744].rearrange("p (h d) -> p h d", d=64),
                                                     func=AF.Copy), r=[z], w=[v_])
            pa = PS[7]
            pav = pa.ap.bitcast(BF16)
            srcs_q = [(qa, 0), (qa, 128), (bqo, 0), (bqo, 128), (bqo, 256), (qd, 0), (qd, 128)]
            bqo2 = bqo[:, :, :].rearrange("p h d -> p (h d)")
            bko2 = bko[:, :, :].rearrange("p h d -> p (h d)")
            for c, (sb_, off) in enumerate(srcs_q):
                sv = bqo2 if sb_ is bqo else sb_[:, :]
                K.pe(lambda e, c=c, sv=sv, off=off: e.transpose(out=pav[:, c * 128:(c + 1) * 128], in_=sv[:, off:off + 128],
                                                                identity=ident_bf[:, :]), r=[sb_, ident_bf], w=[pa])
            K.act(lambda e, q_=q_: e.activation(out=q_[:, :, :], in_=pav[:, 0:896].rearrange("p (c t) -> p c t", t=128), func=AF.Copy),
                  r=[pa], w=[q_])
            pb = PS[4]
            pbv = pb.ap.bitcast(BF16)
            srcs_k = [(qa, 256), (qa, 384), (bko, 0), (bko, 128), (bko, 256), (qd, 256)]
            for c, (sb_, off) in enumerate(srcs_k):
                sv = bko2 if sb_ is bko else sb_[:, :]
                K.pe(lambda e, c=c, sv=sv, off=off: e.transpose(out=pbv[:, c * 128:(c + 1) * 128], in_=sv[:, off:off + 128],
                                                                identity=ident_bf[:, :]), r=[sb_, ident_bf], w=[pb])
            K.dve(lambda e, k_=k_: e.tensor_copy(out=k_[:, :, :], in_=pbv[:, 0:768].rearrange("p (c t) -> p c t", t=128)),
                  r=[pb], w=[k_])
            K.dma("sp", QT[:, blk * 128:(blk + 1) * 128].rearrange("(c p) t -> p c t", p=128), q_[:, :, :], r=[q_],
                  w=[K.dr(("QT", blk))], ds=q_.ds)
            if kind == 1:
                K.dma("sp", KTC[:, blk * 128:(blk + 1) * 128].rearrange("(c p) t -> p c t", p=128), k_[:, :, :], r=[k_],
                      w=[K.dr("KTC")], ds=k_.ds)
                K.dma("sp", VC[:, blk * 65:(blk + 1) * 65].rearrange("(h p) c -> p h c", p=128), v_[:, :, :], r=[v_],
                      w=[K.dr("VC")], ds=v_.ds)
            else:
                lbk = blk - NBC
                K.dma("sp", KTL[:, lbk * 128:(lbk + 1) * 128].rearrange("(c p) t -> p c t", p=128), k_[:, :, :], r=[k_],
                      w=[K.dr("KTL")], ds=k_.ds)
                K.dma("sp", VL[:, lbk * 65:(lbk + 1) * 65].rearrange("(h p) c -> p h c", p=128), v_[:, :, :], r=[v_],
                      w=[K.dr("VL")], ds=v_.ds)
        K.release(m0)
        for a, b in zip(KPB[:-1], KPB[1:]):
            allgather(KTL[a:b, :], KTG[4 * a:4 * b, :], ["KTL"], ["KTG"])
        for hv in range(10):
            allgather(VL[hv * 128:(hv + 1) * 128, :], VG[hv * 512:(hv + 1) * 512, :], ["VL"], ["VG"])
        for k_ in ("KTG", "VG"):
            K.dr(k_).w = (ccsem, ccsem.count)

    def hgrn_phase(l):
        m0 = K.mark()
        OHs = K.alloc("OHs", [64, NCH, 256], F32, dma=True)
        lbt = K.alloc("lbt", [128, 2, 2, 512], F32)
        lb_in = K.alloc("lb_in", [128, 2, L, 512], F32, dma=True)
        K.dma("sp", lb_in[:, :, :, :], lbrow_d, w=[lb_in], ds=lb_in.ds)
        if l == 0:
            K.dve(lambda e: e.memset(lbt[:, :, 0, :], 0.0), w=[lbt])
            K.dve(lambda e: e.memset(lbt[:, :, 1, :], 1.0), w=[lbt])
        else:
            assert L == 2 and l == 1
            K.dve(lambda e: e.tensor_tensor(out=lbt[:, :, 0, :], in0=lb_in[:, :, 1, :], in1=lb_in[:, :, 0, :], op=ALU.subtract),
                  r=[lb_in], w=[lbt])
            K.act(lambda e: e.activation(out=lbt[:, :, 0, :], in_=lbt[:, :, 0, :], func=AF.Sigmoid), r=[lbt], w=[lbt])
            K.dve(lambda e: e.tensor_scalar(out=lbt[:, :, 1, :], in0=lbt[:, :, 0, :], scalar1=-1.0, scalar2=1.0,
                                            op0=ALU.mult, op1=ALU.add), r=[lbt], w=[lbt])
        Sst = K.alloc("Sst", [128, 4, 64], F32)
        Sbf = K.alloc("Sbf", [128, 4, 64], BF16)
        Plog = K.alloc("Plog", [128, 4], F32)
        ePl = K.alloc("ePl", [128, 4], F32)
        dec = K.alloc("dec", [128, 4], F32)
        HS = K.alloc("HS", [128, 2, 260], F32, dma=True)
        Sctx = K.alloc("Sctx", [128, 2, 256], F32)
        zq = [K.alloc(f"zq{i}", [64, 512], F32, dma=True) for i in range(2)]
        zx = [K.alloc(f"zx{i}", [64, 512], F32, dma=True) for i in range(2)]
        zv = [K.alloc(f"zv{i}", [64, 256], F32, dma=True) for i in range(2)]
        sg = K.alloc("hsg", [64, 512], F32)
        qs = K.alloc("hqs", [64, 512], F32)
        ff = K.alloc("hff", [64, 512], F32)
        logf = K.alloc("hlogf", [64, 512], F32)
        kk = K.alloc("hkk", [64, 512], F32)
        ex = [K.alloc(f"hex{i}", [64, 512], F32) for i in range(4)]
        ops_ = [K.alloc(f"hop{i}", [64, 512], BF16) for i in range(4)]
        vb = K.alloc("hvb", [64, 256], BF16)
        TTs = K.alloc("hTT", [128, 12, 64], BF16)
        d1c = K.alloc("hd1c", [64, 512], F32)
        attL = K.alloc("hattL", [32, 4, 64], BF16)
        attH = K.alloc("hattH", [32, 4, 64], BF16)
        zvH = [K.alloc(f"zvH{i}", [32, 256], F32, dma=True) for i in range(2)]
        vbH = K.alloc("hvbH", [32, 256], BF16)
        mk = K.alloc("hmk", [32, 2, 2, 64], BF16)
        mkf = K.alloc("hmkf", [32, 2, 2, 64], F32, dma=True)
        for d2 in range(2):
            for hf in range(2):
                K.dma("sp", mkf[:, d2, hf, :], hmat_d[d2, hf * 32:(hf + 1) * 32, 3, :], w=[mkf], ds=mkf.ds)
        K.dve(lambda e: e.tensor_copy(out=mk[:, :, :, :], in_=mkf[:, :, :, :]), r=[mkf], w=[mk])
        qgs = [K.alloc(f"hqgs{i}", [128, 4, 64], BF16, dma=True) for i in range(2)]
        cnt = 0
        for d_ in range(2):
            K.dve(lambda e: e.memset(Sst[:, :, :], 0.0), w=[Sst])
            K.dve(lambda e: e.memset(Sbf[:, :, :], 0.0), w=[Sbf])
            order = list(range(NCH)) if d_ == 0 else (list(range(2 * NBC - 1, -1, -1)) + list(range(NCH - 1, 2 * NBC - 1, -1)))
            xoff = 512 if d_ == 0 else 1024
            for ci, c in enumerate(order):
                if ci == 2 * NBC:
                    K.dve(lambda e, d_=d_: e.tensor_copy(out=Sctx[:, d_, :], in_=Sst[:, :, :].rearrange("p h v -> p (h v)")),
                          r=[Sst], w=[Sctx])
                    K.dve(lambda e: e.memset(Sst[:, :, :], 0.0), w=[Sst])
                    K.dve(lambda e: e.memset(Sbf[:, :, :], 0.0), w=[Sbf])
                    K.dve(lambda e: e.memset(Plog[:, :], 0.0), w=[Plog])
                    K.dve(lambda e: e.memset(ePl[:, :], 1.0), w=[ePl])
                latent = ci >= 2 * NBC
                q_, x_, v_ = zq[cnt % 2], zx[cnt % 2], zv[cnt % 2]
                cnt += 1
                rows = slice(c * 64, (c + 1) * 64)
                zr = K.dr(("ZH", c // 2))
                K.dma("sp", q_[:, :], ZH[rows, 0:512], r=[zr], w=[q_], ds=q_.ds)
                K.dma("sp", x_[:, :], ZH[rows, xoff:xoff + 512], r=[zr], w=[x_], ds=x_.ds)
                K.dma("sp", v_[:, :], ZH[rows, 1536:1792], r=[zr], w=[v_], ds=v_.ds)
                vh_ = zvH[cnt % 2]
                K.dma("sp", vh_[:, :], ZH[c * 64 + 32:(c + 1) * 64, 1536:1792], r=[zr], w=[vh_], ds=vh_.ds)
                K.pool(lambda e, vh_=vh_: e.tensor_copy(out=vbH[:, :], in_=vh_[:, :]), r=[vh_], w=[vbH])
                K.act(lambda e, q_=q_: e.activation(out=sg[:, :], in_=q_[:, :], func=AF.Sigmoid), r=[q_], w=[sg])
                K.dve(lambda e, q_=q_: e.tensor_tensor(out=qs[:, :], in0=q_[:, :], in1=sg[:, :], op=ALU.mult), r=[q_, sg], w=[qs])
                K.act(lambda e, x_=x_: e.activation(out=ff[:, :], in_=x_[:, :], func=AF.Sigmoid), r=[x_], w=[ff])
                K.pool(lambda e, d_=d_: e.tensor_tensor(out=ff[:, :], in0=ff[:, :], in1=lbt[0:64, d_, 1, :], op=ALU.mult), r=[ff, lbt], w=[ff])
                K.pool(lambda e, d_=d_: e.tensor_tensor(out=ff[:, :], in0=ff[:, :], in1=lbt[0:64, d_, 0, :], op=ALU.add), r=[ff, lbt], w=[ff])
                K.act(lambda e: e.activation(out=logf[:, :], in_=ff[:, :], func=AF.Ln), r=[ff], w=[logf])
                K.pool(lambda e: e.tensor_scalar(out=kk[:, :], in0=ff[:, :], scalar1=-1.0, scalar2=1.0, op0=ALU.mult, op1=ALU.add),
                       r=[ff], w=[kk])
                K.pool(lambda e, v_=v_: e.tensor_copy(out=vb[:, :], in_=v_[:, :]), r=[v_], w=[vb])
                pd = [PS[0], PS[1], PS[2]]
                for m in range(3):
                    K.pe(lambda e, m=m, d_=d_: e.matmul(pd[m][0:64, :], lhsT=hmat[:, d_, m, :], rhs=logf[:, :], start=True, stop=True),
                         r=[hmat, logf], w=[pd[m]])
                K.dve(lambda e: e.tensor_scalar(out=d1c[:, :], in0=pd[0][0:64, :], scalar1=-80.0, scalar2=80.0, op0=ALU.max, op1=ALU.min),
                      r=[pd[0]], w=[d1c])
                K.act(lambda e: e.activation(out=ex[0][:, :], in_=d1c[:, :], func=AF.Exp), r=[d1c], w=[ex[0]])
                K.act(lambda e: e.activation(out=ex[1][:, :], in_=d1c[:, :], func=AF.Exp, scale=-1.0), r=[d1c], w=[ex[1]])
                K.act(lambda e: e.activation(out=ex[2][:, :], in_=pd[2][0:64, :], func=AF.Exp), r=[pd[2]], w=[ex[2]])
                K.act(lambda e: e.activation(out=ex[3][:, :], in_=pd[1][0:64, :], func=AF.Exp), r=[pd[1]], w=[ex[3]])
                K.dve(lambda e: e.tensor_tensor(out=ops_[0][:, :], in0=qs[:, :], in1=ex[0][:, :], op=ALU.mult), r=[qs, ex[0]], w=[ops_[0]])
                K.dve(lambda e: e.tensor_tensor(out=ops_[1][:, :], in0=kk[:, :], in1=ex[1][:, :], op=ALU.mult), r=[kk, ex[1]], w=[ops_[1]])
                K.dve(lambda e: e.tensor_tensor(out=ops_[2][:, :], in0=qs[:, :], in1=ex[2][:, :], op=ALU.mult), r=[qs, ex[2]], w=[ops_[2]])
                K.pool(lambda e: e.tensor_tensor(out=ops_[3][:, :], in0=kk[:, :], in1=ex[3][:, :], op=ALU.mult), r=[kk, ex[3]], w=[ops_[3]])
                ptt = PS[3]
                pttv = ptt.ap.bitcast(BF16)
                for a in range(3):
                    for h in range(4):
                        j = a * 4 + h
                        K.pe(lambda e, a=a, h=h, j=j: e.transpose(out=pttv[:, j * 64:(j + 1) * 64], in_=ops_[a][:, h * 128:(h + 1) * 128],
                                                                  identity=ident_bf[0:64, 0:64]), r=[ops_[a], ident_bf], w=[ptt])
                K.dve(lambda e: e.tensor_copy(out=TTs[:, :, :], in_=pttv[:, 0:768].rearrange("p (j t) -> p j t", t=64)), r=[ptt], w=[TTs])
                pat = PS[4]
                if ci == 0:
                    K.dve(lambda e: e.memset(attL[:, :, :], 0.0), w=[attL])
                    K.dve(lambda e: e.memset(attH[:, :, :], 0.0), w=[attH])
                for h in range(4):
                    if d_ == 0:
                        K.pe(lambda e, h=h: e.matmul(pat[0:32, h * 64:(h + 1) * 64], lhsT=TTs[:, 4 + h, 0:32], rhs=TTs[:, h, 0:64], start=True, stop=True),
                             r=[TTs], w=[pat])
                        K.pe(lambda e, h=h: e.matmul(pat[0:32, 256 + h * 64 + 32:256 + (h + 1) * 64], lhsT=TTs[:, 4 + h, 32:64], rhs=TTs[:, h, 32:64],
                                                     start=True, stop=True), r=[TTs], w=[pat])
                    else:
                        K.pe(lambda e, h=h: e.matmul(pat[0:32, 256 + h * 64:256 + (h + 1) * 64], lhsT=TTs[:, 4 + h, 32:64], rhs=TTs[:, h, 0:64],
                                                     start=True, stop=True), r=[TTs], w=[pat])
                        K.pe(lambda e, h=h: e.matmul(pat[0:32, h * 64:h * 64 + 32], lhsT=TTs[:, 4 + h, 0:32], rhs=TTs[:, h, 0:32],
                                                     start=True, stop=True), r=[TTs], w=[pat])
                pL = pat[0:32, 0:256].rearrange("p (h t) -> p h t", t=64)
                pH = pat[0:32, 256:512].rearrange("p (h t) -> p h t", t=64)
                if d_ == 0:
                    K.dve(lambda e, d_=d_: e.tensor_tensor(out=attL[:, :, :], in0=pL, in1=mk[:, d_, 0, :].unsqueeze(1).broadcast_to([32, 4, 64]),
                                                           op=ALU.mult), r=[pat, mk], w=[attL])
                    K.dve(lambda e, d_=d_: e.tensor_tensor(out=attH[:, :, 32:64], in0=pH[:, :, 32:64],
                                                           in1=mk[:, d_, 1, 32:64].unsqueeze(1).broadcast_to([32, 4, 32]), op=ALU.mult),
                          r=[pat, mk], w=[attH])
                else:
                    K.dve(lambda e, d_=d_: e.tensor_tensor(out=attH[:, :, :], in0=pH, in1=mk[:, d_, 1, :].unsqueeze(1).broadcast_to([32, 4, 64]),
                                                           op=ALU.mult), r=[pat, mk], w=[attH])
                    K.dve(lambda e, d_=d_: e.tensor_tensor(out=attL[:, :, 0:32], in0=pL[:, :, 0:32],
                                                           in1=mk[:, d_, 0, 0:32].unsqueeze(1).broadcast_to([32, 4, 32]), op=ALU.mult),
                          r=[pat, mk], w=[attL])
                po = PS[5]
                for h in range(4):
                    K.pe(lambda e, h=h: e.matmul(po[0:64, h * 64:(h + 1) * 64], lhsT=attL[:, h, :], rhs=vb[0:32, h * 64:(h + 1) * 64],
                                                 start=True, stop=False), r=[attL, vb], w=[po])
                    K.pe(lambda e, h=h: e.matmul(po[0:64, h * 64:(h + 1) * 64], lhsT=attH[:, h, :], rhs=vbH[:, h * 64:(h + 1) * 64],
                                                 start=False, stop=False), r=[attH, vbH], w=[po])
                    K.pe(lambda e, h=h: e.matmul(po[0:64, h * 64:(h + 1) * 64], lhsT=TTs[:, 8 + h, :], rhs=Sbf[:, h, :],
                                                 start=False, stop=True), r=[TTs, Sbf], w=[po])
                if d_ == 0:
                    K.act(lambda e, c=c: e.activation(out=OHs[:, c, :], in_=po[0:64, 0:256], func=AF.Copy), r=[po], w=[OHs])
                else:
                    K.dve(lambda e, c=c: e.tensor_tensor(out=OHs[:, c, :], in0=OHs[:, c, :], in1=po[0:64, 0:256], op=ALU.add),
                          r=[po, OHs], w=[OHs])
                if latent:
                    g_ = qgs[cnt % 2]
                    for h in range(4):
                        K.act(lambda e, h=h, g_=g_: e.activation(out=g_[:, h, :], in_=TTs[:, 8 + h, :], func=AF.Copy, scale=ePl[:, h:h + 1]),
                              r=[TTs, ePl], w=[g_])
                    K.dma("sp", QGS[d_, :, :, c * 64:(c + 1) * 64].rearrange("h k t -> k h t"), g_[:, :, :], r=[g_],
                          w=[K.dr(("QGS", d_, c))], ds=g_.ds)
                pu = PS[6]
                pg = PS[7]
                for h in range(4):
                    K.pe(lambda e, h=h: e.matmul(pu[:, h * 64:(h + 1) * 64], lhsT=ops_[3][:, h * 128:(h + 1) * 128], rhs=vb[:, h * 64:(h + 1) * 64],
                                                 start=True, stop=True), r=[ops_[3], vb], w=[pu])
                    K.pe(lambda e, h=h: e.matmul(pg[:, h:h + 1], lhsT=logf[:, h * 128:(h + 1) * 128], rhs=ones64[:, :], start=True, stop=True),
                         r=[logf, ones64], w=[pg])
                K.act(lambda e: e.activation(out=dec[:, :], in_=pg[:, 0:4], func=AF.Exp), r=[pg], w=[dec])
                for h in range(4):
                    K.dve(lambda e, h=h: e.scalar_tensor_tensor(out=Sst[:, h, :], in0=Sst[:, h, :], scalar=dec[:, h:h + 1],
                                                                in1=pu[:, h * 64:(h + 1) * 64], op0=ALU.mult, op1=ALU.add),
                          r=[Sst, dec, pu], w=[Sst])
                K.pool(lambda e: e.tensor_copy(out=Sbf[:, :, :], in_=Sst[:, :, :]), r=[Sst], w=[Sbf])
                if latent:
                    K.dve(lambda e: e.tensor_tensor(out=Plog[:, :], in0=Plog[:, :], in1=pg[:, 0:4], op=ALU.add), r=[Plog, pg], w=[Plog])
                    K.act(lambda e: e.activation(out=ePl[:, :], in_=Plog[:, :], func=AF.Exp), r=[Plog], w=[ePl])
            K.dve(lambda e, d_=d_: e.tensor_copy(out=HS[:, d_, 0:256], in_=Sst[:, :, :].rearrange("p h v -> p (h v)")), r=[Sst], w=[HS])
            K.dve(lambda e, d_=d_: e.tensor_copy(out=HS[:, d_, 256:260], in_=Plog[:, :]), r=[Plog], w=[HS])
        K.dma("sp", HSL[:, :], HS[:, :, :].rearrange("p d c -> p (d c)"), r=[HS], w=[K.dr("HSL")], ds=HS.ds)
        for c0 in range(0, NCH, 8):
            c1 = min(NCH, c0 + 8)
            K.dma("sp", OH[c0 * 64:c1 * 64, :].rearrange("(c p) n -> p c n", p=64), OHs[:, c0:c1, :], r=[OHs], w=[K.dr("OH")], ds=OHs.ds)
        allgather(HSL, HSG, ["HSL"], ["HSG"], ccsem2)
        hg = K.alloc("hg", [128, 4, 520], F32, dma=True)
        K.dma("sp", hg[:, :, :], HSG.rearrange("(r p) c -> p r c", p=128), r=[K.dr("HSG")], w=[hg], ds=hg.ds)
        eg = K.alloc("eg", [128, 4], F32)
        for d_ in range(2):
            K.dve(lambda e, d_=d_: e.tensor_copy(out=Sst[:, :, :].rearrange("p h v -> p (h v)"), in_=Sctx[:, d_, :]), r=[Sctx], w=[Sst])
            rorder = range(4) if d_ == 0 else range(3, -1, -1)
            for r_ in rorder:
                mcol = segm[:, d_ * 4 + r_: d_ * 4 + r_ + 1]
                K.act(lambda e, d_=d_, r_=r_: e.activation(out=eg[:, :], in_=hg[:, r_, d_ * 260 + 256: d_ * 260 + 260], func=AF.Exp), r=[hg], w=[eg])
                K.dve(lambda e, mcol=mcol: e.tensor_scalar(out=eg[:, :], in0=eg[:, :], scalar1=-1.0, scalar2=mcol, op0=ALU.add, op1=ALU.mult),
                      r=[eg, segm], w=[eg])
                K.dve(lambda e: e.tensor_scalar(out=eg[:, :], in0=eg[:, :], scalar1=1.0, scalar2=None, op0=ALU.add), r=[eg], w=[eg])
                for h in range(4):
                    K.dve(lambda e, h=h: e.tensor_scalar(out=Sst[:, h, :], in0=Sst[:, h, :], scalar1=eg[:, h:h + 1], scalar2=None, op0=ALU.mult),
                          r=[Sst, eg], w=[Sst])
                    K.dve(lambda e, h=h, d_=d_, r_=r_, mcol=mcol: e.scalar_tensor_tensor(
                        out=Sst[:, h, :], in0=hg[:, r_, d_ * 260 + h * 64: d_ * 260 + (h + 1) * 64], scalar=mcol, in1=Sst[:, h, :],
                        op0=ALU.mult, op1=ALU.add), r=[Sst, hg, segm], w=[Sst])
            K.dve(lambda e, d_=d_: e.tensor_copy(out=S0bf[:, d_, :, :], in_=Sst[:, :, :]), r=[Sst], w=[S0bf])
        K.release(m0)

    S0bf = K.alloc("S0bf", [128, 2, 4, 64], BF16)

    def attn_phase(l, need_ctx):
        m0 = K.mark()
        NKC = NBC * 128
        NKEY = NKC + 4 * T
        QTW = min(512, T)
        KTs = [K.alloc(f"KTs{i}", [128, NKEY], BF16, dma=True) for i in range(2)]
        Vs = [K.alloc(f"Vs{i}", [128, NKB, 65], BF16, dma=True) for i in range(2)]
        Qs = [K.alloc(f"Qs{i}", [128, NTOK], BF16, dma=True) for i in range(2)]
        PT = [K.alloc(f"PT{i}", [128, 512], BF16) for i in range(4)]
        Osb = [K.alloc(f"Osb{i}", [65, 512], F32) for i in range(2)]
        rec = [K.alloc(f"rec{i}", [128, 4], F32) for i in range(2)]
        ob = [K.alloc(f"ob{i}", [128, 4, 64], F32, dma=True) for i in range(2)]
        jobs = []
        for h in range(4):
            jobs.append((64 * h, 64, [(64 * h, 64, 0)], h, [(0, 32, 2 * h), (32, 32, 2 * h + 1)]))
        for h in range(4):
            jobs.append((256 + 96 * h, 96, [(256 + 96 * h, 96, 0)], 4 + h, [(0, 96, 8 + h)]))
        for g in range(2):
            jobs.append((640 + 128 * g, 128, [(640 + 64 * g, 64, 0), (640 + 64 * g, 64, 64)], 8 + g, [(0, 64, 12 + 2 * g), (64, 64, 13 + 2 * g)]))
        pcount = 0
        for ji, (qrow, qn_rows, ksrcs, vidx, heads) in enumerate(jobs):
            Kt = KTs[ji % 2]
            Vt = Vs[ji % 2]
            Qt = Qs[ji % 2]
            for (krow, kd, pbase) in ksrcs:
                K.dma("sp", Kt[pbase:pbase + kd, 0:NKC], KTC[krow:krow + kd, :], r=[K.dr("KTC")], w=[Kt], ds=Kt.ds)
                for r_ in range(4):
                    K.dma("sp", Kt[pbase:pbase + kd, NKC + r_ * T: NKC + (r_ + 1) * T], KTG[kt_row(r_, krow): kt_row(r_, krow) + kd, :],
                          r=[K.dr("KTG")], w=[Kt], ds=Kt.ds)
            K.dma("sp", Vt[:, 0:NBC, :].rearrange("p n c -> p (n c)"), VC[vidx * 128:(vidx + 1) * 128, :],
                  r=[K.dr("VC")], w=[Vt], ds=Vt.ds)
            for r_ in range(4):
                K.dma("sp", Vt[:, NBC + r_ * NBL: NBC + (r_ + 1) * NBL, :].rearrange("p n c -> p (n c)"),
                      VG[vidx * 512 + r_ * 128: vidx * 512 + (r_ + 1) * 128, :], r=[K.dr("VG")], w=[Vt], ds=Vt.ds)
            K.dma("sp", Qt[0:qn_rows, :], QT[qrow:qrow + qn_rows, :], r=[K.dr(("QT", b_)) for b_ in range(NB)], w=[Qt], ds=Qt.ds)
            qtiles = [(NKC + t0, min(QTW, T - t0), 0, NKB) for t0 in range(0, T, QTW)]
            if need_ctx:
                qtiles = [(0, NKC, 0, NBC)] + qtiles
            nh = len(heads)
            for (q0, qn, kb0, kb1) in qtiles:
                LA = 2 if nh == 1 else 1
                kbs = list(range(kb0, kb1))
                slots = []
                for i in range(len(kbs) + LA):
                    if i < len(kbs):
                        kb = kbs[i]
                        sl = []
                        for (pb, d, u) in heads:
                            ps = PS[pcount % 4]
                            pt_ = PT[pcount % 4]
                            pcount += 1
                            sl.append((ps, pt_))
                            K.pe(lambda e, ps=ps, Kt=Kt, Qt=Qt, kb=kb, q0=q0, qn=qn, d=d, pb=pb: e.matmul(
                                ps[:, 0:qn], lhsT=Kt[pb:pb + d, kb * 128:(kb + 1) * 128], rhs=Qt[pb:pb + d, q0:q0 + qn], start=True, stop=True),
                                r=[Kt, Qt], w=[ps])
                        slots.append(sl)
                    if i >= LA:
                        kb = kbs[i - LA]
                        for hh in range(nh):
                            ps, pt_ = slots[i - LA][hh]
                            po = PS[4 + hh]
                            K.act(lambda e, ps=ps, pt_=pt_, qn=qn: e.activation(out=pt_[:, 0:qn], in_=ps[:, 0:qn], func=AF.Exp), r=[ps], w=[pt_])
                            K.pe(lambda e, po=po, Vt=Vt, pt_=pt_, kb=kb, qn=qn, kb0=kb0, kb1=kb1: e.matmul(
                                po[0:65, 0:qn], lhsT=Vt[:, kb, :], rhs=pt_[:, 0:qn], start=(kb == kb0), stop=(kb == kb1 - 1)),
                                r=[Vt, pt_], w=[po])
                for hh, (pb, d, u) in enumerate(heads):
                    po = PS[4 + hh]
                    os_, rc, o_, ptr = Osb[hh], rec[hh], ob[hh], PS[6 + hh]
                    K.dve(lambda e, os_=os_, po=po, qn=qn: e.tensor_copy(out=os_[:, 0:qn], in_=po[0:65, 0:qn]), r=[po], w=[os_])
                    nb_ = qn // 128
                    for j in range(nb_):
                        K.pe(lambda e, j=j, os_=os_, ptr=ptr: e.transpose(out=ptr[:, j * 65:(j + 1) * 65], in_=os_[:, j * 128:(j + 1) * 128],
                                                                         identity=ident_f[0:65, 0:65]), r=[os_, ident_f], w=[ptr])
                    pv_ = ptr[:, 0:nb_ * 65].rearrange("p (j c) -> p j c", c=65)
                    K.dve(lambda e, rc=rc, pv_=pv_, nb_=nb_: e.reciprocal(out=rc[:, 0:nb_], in_=pv_[:, :, 64]), r=[ptr], w=[rc])
                    K.dve(lambda e, rc=rc, pv_=pv_, nb_=nb_, o_=o_: e.tensor_tensor(out=o_[:, 0:nb_, :], in0=pv_[:, :, 0:64],
                                                                                 in1=rc[:, 0:nb_].unsqueeze(2).broadcast_to([128, nb_, 64]),
                                                                                 op=ALU.mult), r=[ptr, rc], w=[o_])
                    K.dma("sp", OU[q0:q0 + qn, u * 64:(u + 1) * 64].rearrange("(j p) c -> p j c", p=128), o_[:, 0:nb_, :], r=[o_],
                          w=[K.dr(("OU", u, q0))], ds=o_.ds)
        K.release(m0)

    def mixout_phase(l, xsrc, xdst, blocks):
        m0 = K.mark()
        load_gate(5, 1.0)
        Wo = K.alloc("Wo", [128, 8, D], BF16, dma="sw")
        for k in range(8):
            K.dma("pool", Wo[:, k, :], mix_w_out[l, k * 128:(k + 1) * 128, :], w=[Wo], ds=Wo.ds)
        ou = [K.alloc(f"ou{i}", [128, 16, 64], F32, dma=True) for i in range(2)]
        oh = [K.alloc(f"oh{i}", [128, 256], F32, dma=True) for i in range(2)]
        gg = [K.alloc(f"gg{i}", [128, 256], F32, dma=True) for i in range(2)]
        qg = [K.alloc(f"qg{i}", [128, 2, 4, 128], BF16, dma=True) for i in range(2)]
        xr = [K.alloc(f"oxr{i}", [128, D], F32, dma=True) for i in range(2)]
        y = K.alloc("y", [128, D], F32)
        ybf = K.alloc("ybf", [128, D], BF16)
        yT = K.alloc("yT", [128, 8, 128], BF16)
        dd = K.alloc("dd", [128, 4, 64], F32)
        sq = K.alloc("osq", [128, 4, 64], F32)
        ssb = K.alloc("ossb", [128, 8], F32)
        sgm = K.alloc("osg", [128, 256], F32)
        tmp = K.alloc("otmp", [128, D], F32)
        for n, blk in enumerate(blocks):
            kind = kind_of(blk)
            u_, h_, g_, q_, r_ = ou[n % 2], oh[n % 2], gg[n % 2], qg[n % 2], xr[n % 2]
            rows = slice(blk * 128, (blk + 1) * 128)
            K.dma("sp", u_[:, :, :].rearrange("p u c -> p (u c)"), OU[rows, :],
                  r=[K.dr(("OU", u, q0)) for u in range(16) for q0 in ([0] if kind == 1 else [NBC * 128 + ((blk - NBC) * 128) // min(512, T) * min(512, T)])],
                  w=[u_], ds=u_.ds)
            K.dma("sp", h_[:, :], OH[rows, :], r=[K.dr("OH")], w=[h_], ds=h_.ds)
            K.dma("sp", g_[:, :], ZH[rows, 1792:2048], r=[K.dr(("ZH", blk))], w=[g_], ds=g_.ds)
            K.dma("sp", r_[:, :], xsrc[rows, :], r=[K.dr((id(xsrc), blk))], w=[r_], ds=r_.ds)
            u4 = u_[:, 0:8, :].rearrange("p (h two) c -> p h two c", two=2)
            K.dve(lambda e, u4=u4: e.scalar_tensor_tensor(out=dd[:, :, :], in0=u4[:, :, 1, :], scalar=neglam[:, 0:1], in1=u4[:, :, 0, :],
                                                          op0=ALU.mult, op1=ALU.add), r=[u_, neglam], w=[dd])
            headnorm(dd[:, :, :], 4, 64, gains[:, G_SUB:G_SUB + 256].rearrange("p (h d) -> p h d", d=64),
                     y[:, 0:256].rearrange("p (h d) -> p h d", d=64), sq[:, :, :], ssb[:, 0:4], ssb, K.dve, [dd], [y, sq])
            K.pool(lambda e, u_=u_: e.tensor_copy(out=y[:, 256:512], in_=u_[:, 8:12, :].rearrange("p u c -> p (u c)")), r=[u_], w=[y])
            K.pool(lambda e, u_=u_: e.tensor_copy(out=y[:, 768:1024], in_=u_[:, 12:16, :].rearrange("p u c -> p (u c)")), r=[u_], w=[y])
            if kind == 0:
                for d_ in range(2):
                    K.dma("sp", q_[:, d_, :, :], QGS[d_, :, :, rows].rearrange("h k t -> k h t"),
                          r=[K.dr(("QGS", d_, 2 * blk)), K.dr(("QGS", d_, 2 * blk + 1))], w=[q_], ds=q_.ds)
                pc = PS[0]
                for h in range(4):
                    for d_ in range(2):
                        K.pe(lambda e, h=h, d_=d_, q_=q_: e.matmul(pc[:, h * 64:(h + 1) * 64], lhsT=q_[:, d_, h, :], rhs=S0bf[:, d_, h, :],
                                                                   start=(d_ == 0), stop=(d_ == 1)), r=[q_, S0bf], w=[pc])
                K.dve(lambda e, h_=h_: e.tensor_tensor(out=h_[:, :], in0=h_[:, :], in1=pc[:, 0:256], op=ALU.add), r=[pc, h_], w=[h_])
            headnorm(h_[:, :].rearrange("p (h d) -> p h d", d=64), 4, 64, gains[:, G_HO:G_HO + 256].rearrange("p (h d) -> p h d", d=64),
                     y[:, 512:768].rearrange("p (h d) -> p h d", d=64), sq[:, :, :], ssb[:, 4:8], ssb, K.pool, [h_], [y, sq])
            K.act(lambda e, g_=g_: e.activation(out=sgm[:, :], in_=g_[:, :], func=AF.Sigmoid), r=[g_], w=[sgm])
            K.pool(lambda e, g_=g_: e.tensor_tensor(out=sgm[:, :], in0=sgm[:, :], in1=g_[:, :], op=ALU.mult), r=[sgm, g_], w=[sgm])
            K.pool(lambda e: e.tensor_tensor(out=y[:, 512:768], in0=y[:, 512:768], in1=sgm[:, :], op=ALU.mult), r=[y, sgm], w=[y])
            K.act(lambda e: e.activation(out=ybf[:, :], in_=y[:, :], func=AF.Copy), r=[y], w=[ybf])
            pt = PS[1]
            ptv = pt.ap.bitcast(BF16)
            for c in range(8):
                K.pe(lambda e, c=c: e.transpose(out=ptv[:, c * 128:(c + 1) * 128], in_=ybf[:, c * 128:(c + 1) * 128], identity=ident_bf[:, :]),
                     r=[ybf, ident_bf], w=[pt])
            K.dve(lambda e: e.tensor_copy(out=yT[:, :, :], in_=ptv[:, :].rearrange("p (c t) -> p c t", t=128)), r=[pt], w=[yT])
            for half in range(2):
                py = PS[2 + half]
                for c in range(8):
                    K.pe(lambda e, c=c, py=py, half=half: e.matmul(py[:, :], lhsT=yT[:, c, :], rhs=Wo[:, c, half * 512:(half + 1) * 512],
                                                                  start=(c == 0), stop=(c == 7)), r=[yT, Wo], w=[py])
                K.dve(lambda e, py=py, half=half, kind=kind: e.tensor_tensor(out=tmp[:, half * 512:(half + 1) * 512], in0=py[:, :],
                                                                             in1=gate[kind][:, half * 512:(half + 1) * 512], op=ALU.mult),
                      r=[py, gate[kind]], w=[tmp])
            K.pool(lambda e, r_=r_: e.tensor_tensor(out=r_[:, :], in0=r_[:, :], in1=tmp[:, :], op=ALU.add), r=[r_, tmp], w=[r_])
            K.dma("sp", xdst[rows, :], r_[:, :], r=[r_], w=[K.dr((id(xdst), blk))], ds=r_.ds)
        K.release(m0)

    allblk = list(range(NB))
    latblk = list(range(NBC, NB))
    ph = cfg.phases
    nlayers = L if ph is None else cfg.nlayers_dbg
    xcur = xin
    for l in range(nlayers):
        last = (l == L - 1)
        mod_phase(l)
        layer_consts(l)
        ffn_phase(l, 0, 0, 2, xcur, XA, allblk)
        mixin_phase(l, XA, allblk)
        hgrn_phase(l)
        attn_phase(l, not last)
        blks = latblk if last else allblk
        mixout_phase(l, XA, XB, blks)
        if last:
            ffn_phase(l, 1, 2, 8, XB, None, blks, final_out=True)
        else:
            ffn_phase(l, 1, 2, 8, XB, XA, blks)
        xcur = XA
    if ph is not None:
        cp = K.alloc("cp", [128, D], F32, dma=True)
        srcd = {"XA": XA, "XB": XB, "OU": OU}[cfg.dbg_src]
        for blk in allblk:
            K.dma("sp", cp[:, :], srcd[blk * 128:(blk + 1) * 128, :], r=list(K.dres.values()), w=[cp], ds=cp.ds)
            K.dma("sp", dbg[blk * 128:(blk + 1) * 128, :], cp[:, :], r=[cp], w=[K.dr(("dbg", blk))], ds=cp.ds)
        if not (nlayers == L):
            for lb in range(NBL):
                K.dma("sp", cp[:, :], XA[(NBC + lb) * 128:(NBC + lb + 1) * 128, :], r=list(K.dres.values()), w=[cp], ds=cp.ds)
                K.dma("sp", xout[lb * 128:(lb + 1) * 128, :], cp[:, :], r=[cp], w=[K.dr(("xout", lb))], ds=cp.ds)
    K.wait_all("sp")


def _rope_tables(T, NCTX, j, grid_w=64):
    t = np.arange(j * T, (j + 1) * T)
    r = (t // grid_w).astype(np.float32)
    c = (t % grid_w).astype(np.float32)
    out = np.zeros((NCTX + T, 96), np.float32)
    out[:NCTX, 0:16] = 1.0
    out[:NCTX, 32:64] = 1.0
    for (n_freq, c0, s0) in ((8, 0, 16), (16, 32, 64)):
        inv = (10000.0 ** (-np.arange(n_freq, dtype=np.float32) / n_freq)).astype(np.float32)
        ang = np.concatenate([r[:, None] * inv, c[:, None] * inv], axis=-1).astype(np.float32)
        out[NCTX:, c0:c0 + 2 * n_freq] = np.cos(ang)
        out[NCTX:, s0:s0 + 2 * n_freq] = np.sin(ang)
    return out


def _hmat():
    m = np.zeros((2, 64, 4, 64), np.float32)
    s = np.arange(64)[:, None]
    t = np.arange(64)[None, :]
    for d in range(2):
        MG = (s <= t).astype(np.float32) if d == 0 else (s >= t).astype(np.float32)
        mid = MG[:, 32:33]
        m[d, :, 0, :] = MG - mid
        m[d, :, 1, :] = 1.0 - MG
        m[d, :, 2, :] = MG
        m[d, :, 3, :] = MG
    return m


def make_in_maps(inputs, T, NCTX=256):
    f = lambda a: np.ascontiguousarray(np.asarray(a, dtype=np.float32))
    x, c, ctx, c_ctx = f(inputs["x"]), f(inputs["c"]), f(inputs["ctx"]), f(inputs["c_ctx"])
    L = inputs["ada_w"].shape[0]
    rep = lambda v: np.broadcast_to(np.asarray(v, np.float32).reshape(1, -1), (128, np.asarray(v).size))
    gains = np.zeros((L, 128, 2688), np.float32)
    for l in range(L):
        parts = [np.tile(inputs["da_qk_norm"][l, 0], 8), np.tile(inputs["da_qk_norm"][l, 1], 8),
                 inputs["mla_q_norm"][l], inputs["mla_kv_norm"][l],
                 np.tile(inputs["mla_nope_norm"][l, 0], 4), np.tile(inputs["mla_rope_norm"][l, 0], 4),
                 np.tile(inputs["mla_nope_norm"][l, 1], 4), inputs["mla_rope_norm"][l, 1],
                 np.tile(inputs["gq_qk_norm"][l, 0], 4), np.tile(inputs["gq_qk_norm"][l, 1], 2),
                 np.tile(inputs["da_subln"][l], 4), np.tile(inputs["hg_out_norm"][l], 4)]
        row = np.concatenate([np.asarray(p, np.float32).reshape(-1) for p in parts])
        gains[l, :, :row.size] = row[None, :]
    lamrow = np.stack([rep(inputs["da_lambda"][l]) for l in range(L)])
    lbrow = np.ascontiguousarray(np.broadcast_to(f(inputs["hg_lb_logits"])[None], (128, 2, L, 512)))
    consts = make_consts()
    hm = _hmat()
    shared = dict(ada_w=f(inputs["ada_w"]), ada_b=f(inputs["ada_b"]), norm_w=f(inputs["norm_w"]),
                  ffn_w_in=f(inputs["ffn_w_in"]), ffn_w_out=f(inputs["ffn_w_out"]), mix_w_in=f(inputs["mix_w_in"]),
                  mix_w_out=f(inputs["mix_w_out"]), mla_w_uq=f(inputs["mla_w_uq"]), mla_w_ukv=f(inputs["mla_w_ukv"]),
                  gains=gains, lamrow=np.ascontiguousarray(lamrow), lbrow=lbrow, hmat=hm,
                  ident_bf=consts["ident_bf"], ident_f=consts["ident_f"], sel=consts["sel"])
    maps = []
    for core in range(NCORES):
        b, j = core // 4, core % 4
        segm = np.zeros((128, 8), np.float32)
        for r in range(4):
            segm[:, r] = 1.0 if r < j else 0.0
            segm[:, 4 + r] = 1.0 if r > j else 0.0
        m = dict(shared)
        m.update(xin=np.ascontiguousarray(np.concatenate([ctx[b], x[b, j * T:(j + 1) * T]], 0)),
                 cloc=np.ascontiguousarray(np.stack([c[b], c_ctx])), rope=_rope_tables(T, NCTX, j), segm=segm)
        maps.append(m)
    return maps


_NC_CACHE = {}


def kernel(**inputs):
    B, S, _ = inputs["x"].shape
    T = S // 4
    cfg = Cfg(T=T, NCTX=inputs["ctx"].shape[1], depth=inputs["ada_w"].shape[0])
    nc = build(cfg)
    maps = make_in_maps(inputs, T, cfg.NCTX)
    res = run_bass_kernel_spmd(nc, maps, core_ids=list(range(NCORES)))
    out = np.zeros((B, S, D), np.float32)
    for core in range(NCORES):
        b, j = core // 4, core % 4
        out[b, j * T:(j + 1) * T] = res.results[core]["xout"]
    return out
```
